# Optimizing a Trainium2 kernel written in Bass

```python
import jax, jax.numpy as jnp
from jax import lax
import numpy as np

D_MODEL = 1024
BATCH = 8
SEQ = 2048
DEPTH = 4
DEC_BATCH = 128
DEC_SEQ = 1
PAST_LEN = 16384
PAGE_SIZE = 128

N_MIXERS = 4
D_PLE = 256
D_FF = 4 * D_MODEL
LN_EPS = 1e-5
DEEPNORM_ALPHA = (2 * DEPTH) ** 0.25
DEEPNORM_BETA = (8 * DEPTH) ** -0.25
CHUNK = 128
D_A = D_MODEL
A_GROUP = 128
A_HEADS = D_A // A_GROUP
POOL_WINDOWS = (2, 4, 8, 16)
POOL_GROUPS = len(POOL_WINDOWS)
D_B = D_MODEL
B_GROUP = D_B // POOL_GROUPS
POOL_BUF = max(POOL_WINDOWS) - 1
D_C = D_MODEL
CONV_K = 31
D_D = D_MODEL
SCONV_K = 3

kernel_name = 'hybrid_chunkmlp_pool_conformer_shortconv_decoder_step'


def layer_norm(x, g, b):
    xf = x.astype(jnp.float32)
    mu = jnp.mean(xf, axis=-1, keepdims=True)
    xc = xf - mu
    var = jnp.mean(xc * xc, axis=-1, keepdims=True)
    y = xc * lax.rsqrt(var + LN_EPS) * g.astype(jnp.float32) + b.astype(jnp.float32)
    return y.astype(x.dtype)


def causal_dwconv(buf, x, w):
    k = w.shape[0]
    ext = jnp.concatenate([buf.astype(x.dtype), x], axis=1)
    y = lax.conv_general_dilated(ext, w.astype(x.dtype)[:, None, :], window_strides=(1,), padding='VALID',
                                 dimension_numbers=('NWC', 'WIO', 'NWC'), feature_group_count=x.shape[-1])
    return y, ext[:, ext.shape[1] - (k - 1):]


def mixer_chunk_mlp(x, w_in, b_in, ln_g, ln_b, w_s, b_s, w_out):
    z = jax.nn.gelu(x @ w_in + b_in)
    u, v = jnp.split(z, 2, axis=-1)
    vn = layer_norm(v, ln_g, ln_b)
    bsz, length, _ = vn.shape
    n_chunks = -(-length // CHUNK)
    pad = n_chunks * CHUNK - length
    vp = jnp.pad(vn, ((0, 0), (0, pad), (0, 0))).reshape(bsz, n_chunks, CHUNK, A_HEADS, A_GROUP)
    mask = jnp.tril(jnp.ones((CHUNK, CHUNK), dtype=bool))
    ws = jnp.where(mask[None], w_s, jnp.zeros_like(w_s))
    s = jnp.einsum('hij,bcjhd->bcihd', ws, vp) + b_s.T[None, None, :, :, None]
    s = s.reshape(bsz, n_chunks * CHUNK, D_A)[:, :length]
    out = (u * s) @ w_out
    last_start = ((length - 1) // CHUNK) * CHUNK
    return out, vn[:, last_start:]


def mixer_pool(x, buf, pos0, w_in, w_grp, scale, w_out):
    h = x @ w_in
    bsz, length, _ = h.shape
    ext = jnp.concatenate([buf.astype(h.dtype), h], axis=1)
    cs = jnp.pad(jnp.cumsum(ext.astype(jnp.float32), axis=1), ((0, 0), (1, 0), (0, 0)))
    pos = pos0 + jnp.arange(length)
    pieces = []
    for g, w in enumerate(POOL_WINDOWS):
        sl = slice(g * B_GROUP, (g + 1) * B_GROUP)
        end = cs[:, POOL_BUF + 1:POOL_BUF + 1 + length, sl]
        start = cs[:, POOL_BUF + 1 - w:POOL_BUF + 1 - w + length, sl]
        cnt = jnp.minimum(w, pos + 1).astype(jnp.float32)[None, :, None]
        pieces.append((end - start) / cnt)
    pooled = jnp.concatenate(pieces, axis=-1).astype(h.dtype) - h
    mixed = jnp.einsum('blgc,gcd->blgd', pooled.reshape(bsz, length, POOL_GROUPS, B_GROUP), w_grp)
    out = (mixed.reshape(bsz, length, D_B) * scale) @ w_out
    return out, ext[:, ext.shape[1] - POOL_BUF:]


def mixer_conformer_conv(x, buf, w_in, b_in, w_dw, b_dw, ln_g, ln_b, w_out):
    a, gate = jnp.split(x @ w_in + b_in, 2, axis=-1)
    g = a * jax.nn.sigmoid(gate)
    y, new_buf = causal_dwconv(buf, g, w_dw)
    y = jax.nn.silu(layer_norm(y + b_dw, ln_g, ln_b))
    return y @ w_out, new_buf


def mixer_short_conv(x, buf, w_in, w_conv, w_out):
    bg, cg, h = jnp.split(x @ w_in, 3, axis=-1)
    y, new_buf = causal_dwconv(buf, cg * h, w_conv)
    return (bg * y) @ w_out, new_buf


def trunk(x, p, pool_buf, conv_buf, sconv_buf, pos0, prm):
    chunk_v = pool_new = conv_new = sconv_new = None
    for i in range(DEPTH):
        kind = i % N_MIXERS
        if kind == 0:
            mix, chunk_v = mixer_chunk_mlp(x, prm['a_w_in'], prm['a_b_in'], prm['a_ln_g'], prm['a_ln_b'],
                                           prm['a_w_s'], prm['a_b_s'], prm['a_w_out'])
        elif kind == 1:
            mix, pool_new = mixer_pool(x, pool_buf, pos0, prm['b_w_in'], prm['b_w_grp'], prm['b_scale'], prm['b_w_out'])
        elif kind == 2:
            mix, conv_new = mixer_conformer_conv(x, conv_buf, prm['c_w_in'], prm['c_b_in'], prm['c_w_dw'], prm['c_b_dw'],
                                                 prm['c_ln_g'], prm['c_ln_b'], prm['c_w_out'])
        else:
            mix, sconv_new = mixer_short_conv(x, sconv_buf, prm['d_w_in'], prm['d_w_conv'], prm['d_w_out'])
        x = layer_norm(DEEPNORM_ALPHA * x + mix, prm['ln1_g'][i], prm['ln1_b'][i])
        hid = jnp.square(jax.nn.relu(x @ prm['mlp_w_up'][i]))
        x = layer_norm(DEEPNORM_ALPHA * x + hid @ prm['mlp_w_down'][i], prm['ln2_g'][i], prm['ln2_b'][i])
        x = x + (p[i] @ prm['ple_w_proj'][i]) * jax.nn.sigmoid(x @ prm['ple_w_gate'][i])
    return x, chunk_v, pool_new, conv_new, sconv_new


def setup_inputs(seed: int = 0) -> dict:
    key = jax.random.key(seed)
    ks = iter(jax.random.split(key, 48))
    f32 = jnp.float32

    def nrm(shape, scale):
        return jax.random.normal(next(ks), shape, f32) * scale

    def gain(shape):
        return 1.0 + nrm(shape, 0.05)

    beta = DEEPNORM_BETA
    d = D_MODEL
    return {
        'x_prompt': nrm((BATCH, SEQ, d), 1.0),
        'x_sample': nrm((DEC_BATCH, DEC_SEQ, d), 1.0),
        'state_pool': nrm((DEC_BATCH, POOL_BUF, D_B), 1.0),
        'state_conv': nrm((DEC_BATCH, CONV_K - 1, D_C), 0.5),
        'state_shortconv': nrm((DEC_BATCH, SCONV_K - 1, D_D), 1.0),
        'p_prompt': nrm((DEPTH, BATCH, SEQ, D_PLE), 1.0),
        'p_sample': nrm((DEPTH, DEC_BATCH, DEC_SEQ, D_PLE), 1.0),
        'a_w_in': nrm((d, 2 * D_A), d ** -0.5),
        'a_b_in': nrm((2 * D_A,), 0.02),
        'a_ln_g': gain((D_A,)),
        'a_ln_b': nrm((D_A,), 0.02),
        'a_w_s': nrm((A_HEADS, CHUNK, CHUNK), CHUNK ** -0.5),
        'a_b_s': gain((A_HEADS, CHUNK)),
        'a_w_out': nrm((D_A, d), D_A ** -0.5 * beta),
        'b_w_in': nrm((d, D_B), d ** -0.5),
        'b_w_grp': nrm((POOL_GROUPS, B_GROUP, B_GROUP), B_GROUP ** -0.5),
        'b_scale': gain((D_B,)),
        'b_w_out': nrm((D_B, d), D_B ** -0.5 * beta),
        'c_w_in': nrm((d, 2 * D_C), d ** -0.5),
        'c_b_in': nrm((2 * D_C,), 0.02),
        'c_w_dw': nrm((CONV_K, D_C), CONV_K ** -0.5),
        'c_b_dw': nrm((D_C,), 0.02),
        'c_ln_g': gain((D_C,)),
        'c_ln_b': nrm((D_C,), 0.02),
        'c_w_out': nrm((D_C, d), D_C ** -0.5 * beta),
        'd_w_in': nrm((d, 3 * D_D), d ** -0.5),
        'd_w_conv': nrm((SCONV_K, D_D), SCONV_K ** -0.5),
        'd_w_out': nrm((D_D, d), D_D ** -0.5 * beta),
        'ln1_g': gain((DEPTH, d)),
        'ln1_b': nrm((DEPTH, d), 0.02),
        'ln2_g': gain((DEPTH, d)),
        'ln2_b': nrm((DEPTH, d), 0.02),
        'mlp_w_up': nrm((DEPTH, d, D_FF), d ** -0.5),
        'mlp_w_down': nrm((DEPTH, D_FF, d), D_FF ** -0.5 * beta),
        'ple_w_proj': nrm((DEPTH, D_PLE, d), D_PLE ** -0.5),
        'ple_w_gate': nrm((DEPTH, d, d), d ** -0.5),
    }


def reference(x_prompt, x_sample, state_pool, state_conv, state_shortconv, p_prompt, p_sample,
              a_w_in, a_b_in, a_ln_g, a_ln_b, a_w_s, a_b_s, a_w_out,
              b_w_in, b_w_grp, b_scale, b_w_out,
              c_w_in, c_b_in, c_w_dw, c_b_dw, c_ln_g, c_ln_b, c_w_out,
              d_w_in, d_w_conv, d_w_out,
              ln1_g, ln1_b, ln2_g, ln2_b, mlp_w_up, mlp_w_down, ple_w_proj, ple_w_gate):
    prm = {
        'a_w_in': a_w_in, 'a_b_in': a_b_in, 'a_ln_g': a_ln_g, 'a_ln_b': a_ln_b,
        'a_w_s': a_w_s, 'a_b_s': a_b_s, 'a_w_out': a_w_out,
        'b_w_in': b_w_in, 'b_w_grp': b_w_grp, 'b_scale': b_scale, 'b_w_out': b_w_out,
        'c_w_in': c_w_in, 'c_b_in': c_b_in, 'c_w_dw': c_w_dw, 'c_b_dw': c_b_dw,
        'c_ln_g': c_ln_g, 'c_ln_b': c_ln_b, 'c_w_out': c_w_out,
        'd_w_in': d_w_in, 'd_w_conv': d_w_conv, 'd_w_out': d_w_out,
        'ln1_g': ln1_g, 'ln1_b': ln1_b, 'ln2_g': ln2_g, 'ln2_b': ln2_b,
        'mlp_w_up': mlp_w_up, 'mlp_w_down': mlp_w_down,
        'ple_w_proj': ple_w_proj, 'ple_w_gate': ple_w_gate,
    }
    dt = x_prompt.dtype
    zero_pool = jnp.zeros((BATCH, POOL_BUF, D_B), dt)
    zero_conv = jnp.zeros((BATCH, CONV_K - 1, D_C), dt)
    zero_sconv = jnp.zeros((BATCH, SCONV_K - 1, D_D), dt)
    y_prompt, chunkv_prompt, pool_prompt, conv_prompt, sconv_prompt = trunk(
        x_prompt, p_prompt, zero_pool, zero_conv, zero_sconv, 0, prm)
    y_sample, chunkv_sample, pool_sample, conv_sample, sconv_sample = trunk(
        x_sample, p_sample, state_pool, state_conv, state_shortconv, PAST_LEN, prm)
    return (y_prompt, y_sample, chunkv_prompt, chunkv_sample, pool_prompt, pool_sample,
            conv_prompt, conv_sample, sconv_prompt, sconv_sample)
```

```python
import numpy as np
import concourse.bass as bass
import concourse.mybir as mybir
from concourse.bass_utils import run_bass_kernel_spmd

F32 = mybir.dt.float32
BF16 = mybir.dt.bfloat16
AF = mybir.ActivationFunctionType
ALU = mybir.AluOpType

NCORES = 8
D = 1024
SEQ = 2048
NS = 16
NPB = 1024
NCM = NPB + NS
ALPHA = float(8 ** 0.25)
EPS = 1e-5
BLKB = 64
NSLOT = 5
SKEW = 2
SLOT_ELEMS = 4096

PV_ORDER = [("a_b_in", 16), ("a_ln_g", 8), ("a_ln_b", 8), ("b_scale", 8), ("c_b_in", 16), ("c_b_dw", 8),
            ("c_ln_g", 8), ("c_ln_b", 8), ("ln1_g", 32), ("ln1_b", 32), ("ln2_g", 32), ("ln2_b", 32),
            ("c_w_dw", 248), ("d_w_conv", 24)]
PV_OFF = {}
_o = 0
for _n, _c in PV_ORDER:
    PV_OFF[_n] = _o
    _o += _c
PV_ROWS = _o


class Prog:
    def __init__(self, nc):
        self.nc = nc
        self.ops = []
        self.st = {}
        self.eng = {"pe": nc.tensor, "act": nc.scalar, "dve": nc.vector, "pool": nc.gpsimd, "sp": nc.sync}

    def blocks(self, ap):
        t = ap.tensor
        if type(t).__name__.startswith("DRam"):
            return None
        es = mybir.dt.size(ap.dtype)
        dims = ap.ap
        pstride = dims[0][0]
        off = (ap.offset % pstride) * es
        fd = list(dims[1:])
        starts = np.zeros(1, dtype=np.int64)
        if len(fd) == 0:
            run = es
        else:
            if fd[-1][0] == 1:
                run = fd[-1][1] * es
                outer = fd[:-1]
            else:
                run = es
                outer = fd
            for step, cnt in outer:
                starts = (starts[:, None] + (np.arange(cnt, dtype=np.int64) * step * es)[None, :]).reshape(-1)
        name = t.name
        blkb = 2048 if name == "PS" else BLKB
        lo = (off + starts) // blkb
        hi = (off + starts + run - 1) // blkb
        out = set()
        for l, h in zip(lo.tolist(), hi.tolist()):
            for b in range(l, h + 1):
                out.add((name, b))
        return out

    def add(self, issuer, fn, reads=(), writes=(), dma=False):
        opid = len(self.ops)
        raw = set()
        oth = set()
        rb = set()
        wb = set()
        for ap in reads:
            b = self.blocks(ap)
            if b:
                rb |= b
        for ap in writes:
            b = self.blocks(ap)
            if b:
                wb |= b
        st = self.st
        for k in rb:
            s = st.get(k)
            if s is not None and s[0] is not None:
                raw.add(s[0])
            if s is not None and k[0] == "PS":
                for rk_, rop in s[1].items():
                    if rk_ != issuer:
                        raw.add(rop)
        for k in wb:
            s = st.get(k)
            if s is not None:
                if s[0] is not None:
                    oth.add(s[0])
                oth.update(s[1].values())
        rk = ("dma", opid) if dma else issuer
        for k in rb:
            s = st.get(k)
            if s is None:
                s = st[k] = [None, {}]
            s[1][rk] = opid
        for k in wb:
            st[k] = [opid, {}]
        deps = set()
        for p in raw | oth:
            if p == opid:
                continue
            po = self.ops[p]
            if (not dma) and (not po["dma"]) and po["issuer"] == issuer:
                if issuer == "pe":
                    continue
            deps.add(p)
        self.ops.append(dict(issuer=issuer, fn=fn, deps=deps, dma=dma))
        return opid

    def emit(self):
        nc = self.nc
        ops = self.ops
        needs = [False] * len(ops)
        for o in ops:
            for p in o["deps"]:
                needs[p] = True
        esem = {k: nc.semaphore("s_" + k).__enter__() for k in ("pe", "act", "dve", "pool")}
        ecnt = {k: 0 for k in esem}
        NSP = 8
        dsem = {"sp": [nc.semaphore("d_sp%d" % i).__enter__() for i in range(NSP)],
                "pool": [nc.semaphore("d_pl%d" % i).__enter__() for i in range(8)]}
        dtot = {"sp": [0] * NSP, "pool": [0] * 8}
        dnext = {"sp": 0, "pool": 0}
        waited = {k: {} for k in self.eng}
        inflight = {"sp": [], "pool": []}
        MAXQ = {"sp": 6, "pool": 4}
        events = [None] * len(ops)

        def wait(issuer, sem, val):
            w = waited[issuer]
            key = id(sem)
            if w.get(key, 0) >= val:
                return
            w[key] = val
            self.eng[issuer].wait_ge(sem, val)

        for i, o in enumerate(ops):
            iss = o["issuer"]
            for p in sorted(o["deps"]):
                sem, val = events[p]
                wait(iss, sem, val)
            if o["dma"]:
                fl = inflight[iss]
                if len(fl) >= MAXQ[iss]:
                    wait(iss, *fl[-MAXQ[iss]])
                j = dnext[iss]
                dnext[iss] = (j + 1) % len(dsem[iss])
                sem = dsem[iss][j]
                if dtot[iss][j] > 0:
                    wait(iss, sem, dtot[iss][j])
                ins = o["fn"]()
                dtot[iss][j] += 16
                ins.then_inc(sem, 16)
                events[i] = (sem, dtot[iss][j])
                fl.append(events[i])
            else:
                ins = o["fn"]()
                if needs[i]:
                    ecnt[iss] += 1
                    ins.then_inc(esem[iss], 1)
                    events[i] = (esem[iss], ecnt[iss])
        for q in ("sp", "pool"):
            for j, sem in enumerate(dsem[q]):
                if dtot[q][j] > 0:
                    nc.sync.wait_ge(sem, dtot[q][j])


def build(nlayers=4, stage=9):
    nc = bass.Bass("TRN2", target_bir_lowering=False)
    pg = Prog(nc)

    def din(name, shape):
        return nc.dram_tensor(name, list(shape), F32, kind="ExternalInput").ap()

    def dout(name, shape):
        return nc.dram_tensor(name, list(shape), F32, kind="ExternalOutput").ap()

    def finish():
        with nc.allow_non_contiguous_dma(reason="small strided parameter/state loads"):
            pg.emit()
        return nc

    xp = din("xp", [SEQ, D]); xs = din("xs", [NS, D])
    pp = din("pp", [4, SEQ, 256]); psm = din("psm", [4, NS, 256])
    spool = din("spool", [NS * 15, D]); scv = din("scv", [NS * 30, D]); ssc = din("ssc", [NS * 2, D])
    pvec = din("pvec", [PV_ROWS, 128])
    identd = din("ident", [128, 128]); maskd = din("mask", [128, 128])
    invcd = din("invc", [128, 64]); seld = din("sel", [120, 32])
    a_w_in = din("a_w_in", [D, 2 * D]); a_w_s = din("a_w_s", [8, 128, 128]); a_b_s = din("a_b_s", [1, D])
    a_w_out = din("a_w_out", [D, D])
    b_w_in = din("b_w_in", [D, D]); b_w_grp = din("b_w_grp", [4, 256, 256]); b_w_out = din("b_w_out", [D, D])
    c_w_in = din("c_w_in", [D, 2 * D]); c_w_out = din("c_w_out", [D, D])
    d_w_in = din("d_w_in", [D, 3 * D]); d_w_out = din("d_w_out", [D, D])
    w_up = din("mlp_w_up", [4, D, 4 * D]); w_down = din("mlp_w_down", [4, 4 * D, D])
    w_proj = din("ple_w_proj", [4, 256, D]); w_gate = din("ple_w_gate", [4, D, D])

    yp = dout("yp", [SEQ, D]); ys = dout("ys", [NS, D])
    cvp = dout("cvp", [128, D]); cvs = dout("cvs", [NS, D])
    plp = dout("plp", [15, D]); pls = dout("pls", [NS * 15, D])
    cnp = dout("cnp", [30, D]); cns = dout("cns", [NS * 30, D])
    scp = dout("scp", [2, D]); scs = dout("scs", [NS * 2, D])

    def sb(name, shape, dt=F32):
        return nc.sbuf_tensor(name, list(shape), dt).__enter__()

    xT = sb("xT", [128, 8, NCM]); xTb = sb("xTb", [128, 8, NCM], BF16)
    ring = sb("ring", [128, NSLOT, SLOT_ELEMS], BF16)
    ARENA_B = 67584
    arena = sb("arena", [128, ARENA_B // 4])
    psmp = sb("psmp", [NS, 256]); pT = sb("pT", [128, 2, NCM], BF16)
    zb = sb("zb", [128, 8, 512], BF16); zsq = sb("zsq", [128, 8, 512], BF16)
    msq = sb("msq", [128, 512]); rstd = sb("rstd", [128, 512]); var = msq; sd = msq
    P = sb("P", [128, PV_ROWS])
    ident = sb("identf", [128, 128]); identb = sb("identb", [128, 128], BF16)
    onesb = sb("onesb", [128, 128], BF16); ones1 = sb("ones1", [1, 128], BF16)
    wsT = sb("wsT", [128, 8, 128], BF16)
    bshi = sb("bshi", [1, D], BF16); bslo = sb("bslo", [1, D], BF16)
    ws00 = sb("ws00", [128, 8]); bs0 = sb("bs0", [128, 8])
    invc = sb("invct", [128, 64]); sel = sb("selt", [120, 32])
    epsc = sb("epsc", [128, 1])
    hcar = sb("hcar", [128, 8, 15]); gcar = sb("gcar", [128, 8, 30], BF16); qcar = sb("qcar", [128, 8, 2], BF16)
    tmpf = [sb("tmpf%d" % i, [128, 512]) for i in range(3)]
    ostg = [sb("ostg%d" % i, [128, D]) for i in range(2)]
    smallf = sb("smallf", [128, 8, 32])
    small2 = sb("small2", [128, 8, 32])
    tails = sb("tails", [128, 8, 32])
    PS = nc.psum_tensor("PS", [128, 8, 512], F32).__enter__()

    def aview(byte_off, shape, dt):
        es = mybir.dt.size(dt)
        n = int(np.prod(shape))
        a = arena[:, byte_off // 4: byte_off // 4 + (n * es + 3) // 4]
        if dt != F32:
            a = a.bitcast(dt)
        a = a[:, 0:n]
        if len(shape) == 1:
            return a
        if len(shape) == 2:
            return a.rearrange("p (a b) -> p a b", a=shape[0])
        return a.rearrange("p (a b c) -> p a b c", a=shape[0], b=shape[1])

    E = pg.eng
    cnt = {"ps": 0, "tmp": 0, "ostg": 0, "ws": 0}

    def psn():
        b = cnt["ps"] % 6
        cnt["ps"] += 1
        return b

    def tmpn():
        t = tmpf[cnt["tmp"] % 3]
        cnt["tmp"] += 1
        return t

    def ostgn():
        t = ostg[cnt["ostg"] % 2]
        cnt["ostg"] += 1
        return t

    def isap(x):
        return not isinstance(x, (int, float)) and x is not None

    def act(out, in_, func, bias=None, scale=None):
        kw = {}
        rd = [in_]
        if bias is not None:
            kw["bias"] = bias
            if isap(bias):
                rd.append(bias)
        if scale is not None:
            kw["scale"] = scale
            if isap(scale):
                rd.append(scale)
        pg.add("act", lambda: nc.scalar.activation(out=out, in_=in_, func=func, **kw), rd, [out])

    def tt(out, in0, in1, op, eng="dve"):
        pg.add(eng, lambda: E[eng].tensor_tensor(out=out, in0=in0, in1=in1, op=op), [in0, in1], [out])

    def stt(out, in0, scalar, in1, op0, op1):
        rd = [in0, in1] + ([scalar] if isap(scalar) else [])
        pg.add("dve", lambda: nc.vector.scalar_tensor_tensor(out=out, in0=in0, scalar=scalar, in1=in1, op0=op0, op1=op1),
               rd, [out])

    def ts(out, in0, s1, s2, op0, op1=None, eng="dve"):
        rd = [in0] + [s for s in (s1, s2) if isap(s)]
        if op1 is None:
            pg.add(eng, lambda: E[eng].tensor_scalar(out=out, in0=in0, scalar1=s1, scalar2=None, op0=op0), rd, [out])
        else:
            pg.add(eng, lambda: E[eng].tensor_scalar(out=out, in0=in0, scalar1=s1, scalar2=s2, op0=op0, op1=op1), rd, [out])

    def cp(out, in_, eng="dve"):
        if eng == "act":
            act(out, in_, AF.Copy)
        else:
            pg.add(eng, lambda: E[eng].tensor_copy(out=out, in_=in_), [in_], [out])

    def memset(ap, v, eng="dve"):
        pg.add(eng, lambda: E[eng].memset(ap, v), [], [ap])

    def mm(out, lhsT, rhs, start, stop):
        pg.add("pe", lambda: nc.tensor.matmul(out, lhsT=lhsT, rhs=rhs, start=start, stop=stop), [lhsT, rhs], [out])

    def tr(out, in_, idn):
        pg.add("pe", lambda: nc.tensor.transpose(out=out, in_=in_, identity=idn), [in_, idn], [out])

    def dma(q, out, in_):
        pg.add(q, lambda: E[q].dma_start(out=out, in_=in_), [in_], [out], dma=True)

    def wload(dview):
        s = cnt["ws"] % NSLOT
        cnt["ws"] += 1
        kc, ncols = dview.shape[1], dview.shape[2]
        assert kc * ncols <= SLOT_ELEMS
        dst = ring[:, s, 0:kc * ncols].rearrange("p (k f) -> p k f", k=kc)
        dma("pool", dst, dview)
        return dst

    def pcol(name, idx):
        c = PV_OFF[name] + idx
        return P[:, c:c + 1]

    dma("sp", ident[:], identd)
    mask = aview(16384, [128], F32)
    dma("sp", mask[:, :], maskd)
    dma("sp", invc[:], invcd)
    dma("sp", sel[:], seld)
    memset(onesb[:], 1.0 / 1024.0)
    memset(ones1[:], 1.0)
    memset(epsc[:], EPS)
    cp(identb[:], ident[:])
    if "nosetup" in DBG:
        return finish()
    pst = aview(0, [4, 128], F32)
    for r in range(4):
        dma("sp", pst[0:120, r, :], pvec[r * 120:(r + 1) * 120, :])
    b = psn()
    for r in range(4):
        tr(PS[:, b, r * 120:(r + 1) * 120], pst[0:120, r, :], ident[0:120, 0:120])
    cp(P[:], PS[:, b, 0:PV_ROWS])
    if "nows" in DBG:
        return finish()
    wsl = aview(4096, [8, 128], F32)
    with nc.allow_non_contiguous_dma(reason="small param load"):
        dma("sp", wsl, a_w_s.rearrange("h i j -> i h j"))
        dma("sp", ws00[:], bass.AP(a_w_s.tensor, 0, [[0, 128], [128 * 128, 8]]))
        dma("sp", bs0[:], bass.AP(a_b_s.tensor, 0, [[0, 128], [128, 8]]))
    for h in range(8):
        tt(wsl[:, h, :], wsl[:, h, :], mask[:, :], ALU.mult)
    for hh in range(2):
        b = psn()
        for q in range(4):
            tr(PS[:, b, q * 128:(q + 1) * 128], wsl[:, hh * 4 + q, :], ident[:])
        cp(wsT[:, hh * 4:(hh + 1) * 4, :], PS[:, b, :].rearrange("p (a b) -> p a b", a=4))
    bsf = aview(8192, [D], F32); bsr = aview(12288, [D], F32)
    dma("sp", bsf[0:1, :], a_b_s)
    cp(bshi[:], bsf[0:1, :])
    tt(bsr[0:1, :], bsf[0:1, :], bshi[:], ALU.subtract)
    cp(bslo[:], bsr[0:1, :])

    if "noblocks" in DBG:
        return finish()
    pending = []

    def advance():
        if pending:
            item = pending[0]
            try:
                next(item[1])
            except StopIteration:
                pending.pop(0)

    def flush(c0=None):
        if c0 is None:
            idx = len(pending) - 1
        else:
            idx = -1
            for i, item in enumerate(pending):
                if item[0] == c0:
                    idx = i
        for _ in range(idx + 1):
            item = pending.pop(0)
            for _ in item[1]:
                pass

    def ln_gen(zT, c0, n, gname, bname, pidx, out_f, out_b, func):
        for hh in range(2):
            zv = zT[:, hh * 4:(hh + 1) * 4, c0:c0 + n]
            act(zb[:, hh * 4:(hh + 1) * 4, 0:n], zv, AF.Copy)
            act(zsq[:, hh * 4:(hh + 1) * 4, 0:n], zv, AF.Square)
        yield
        bm = 6
        be = 7
        for k in range(8):
            mm(PS[:, bm, 0:n], onesb[:], zb[:, k, 0:n], k == 0, k == 7)
        for k in range(8):
            mm(PS[:, be, 0:n], onesb[:], zsq[:, k, 0:n], k == 0, k == 7)
        yield
        act(msq[:, 0:n], PS[:, bm, 0:n], AF.Square)
        tt(var[:, 0:n], PS[:, be, 0:n], msq[:, 0:n], ALU.subtract)
        act(sd[:, 0:n], var[:, 0:n], AF.Ln, bias=epsc[:, 0:1])
        act(rstd[:, 0:n], sd[:, 0:n], AF.Exp, scale=-0.5)
        stt(PS[:, bm, 0:n], PS[:, bm, 0:n], -1.0, rstd[:, 0:n], ALU.mult, ALU.mult)
        act(PS[:, be, 0:n], rstd[:, 0:n], AF.Copy)
        yield
        NPOOL = 0
        for k in range(8):
            zk = zT[:, k, c0:c0 + n]
            if k >= 8 - NPOOL:
                tt(zk, zk, rstd[:, 0:n], ALU.mult, eng="pool")
                tt(zk, zk, nmrs[:, 0:n], ALU.add, eng="pool")
            else:
                tt(zk, zk, PS[:, be, 0:n], ALU.mult)
                tt(zk, zk, PS[:, bm, 0:n], ALU.add)
            if k % 2 == 1 and k < 7:
                yield
        yield
        order = [5, 6, 7, 0, 1, 2, 3, 4] if NPOOL else list(range(8))
        for i, k in enumerate(order):
            zk = zT[:, k, c0:c0 + n]
            g = pcol(gname, pidx + k); bb = pcol(bname, pidx + k)
            if func == AF.Identity:
                if out_b is not None:
                    ts(out_b[:, k, c0:c0 + n], zk, g, bb, ALU.mult, ALU.add)
                if out_f is not None:
                    act(out_f[:, k, c0:c0 + n], zk, AF.Identity, bias=bb, scale=g)
            else:
                act(out_b[:, k, c0:c0 + n], zk, func, bias=bb, scale=g)
            if i % 2 == 1 and i < 7:
                yield

    def ln(zT, c0, n, gname, bname, pidx, out_f=None, out_b=None, func=AF.Identity):
        pending.append((c0, ln_gen(zT, c0, n, gname, bname, pidx, out_f, out_b, func)))

    def proj(specs, nchunks, evac, ctiles, tile_chunks=4, after_ct=None, win=None):
        if win is None:
            win = 1
        ngroups = nchunks * len(ctiles)
        adv_every = max(1, ngroups // 10)
        gcount = 0
        steps = list(range(0, nchunks, tile_chunks))
        for w0 in range(0, len(steps), win):
            wsteps = steps[w0:w0 + win]
            last_win = w0 + win >= len(steps)
            wts = {}
            for t0 in wsteps:
                tcn = min(tile_chunks, nchunks - t0)
                wts[t0] = [wload(Wv[:, :, col0 + t0 * 128: col0 + (t0 + tcn) * 128]) for (Wv, col0, kc, rhs_fn) in specs]
            for (c0, n) in ctiles:
                flush(c0)
                for t0 in wsteps:
                    tcn = min(tile_chunks, nchunks - t0)
                    for mi in range(tcn):
                        m = t0 + mi
                        banks = []
                        for si, (Wv, col0, kc, rhs_fn) in enumerate(specs):
                            bk = psn()
                            banks.append(PS[:, bk, 0:n])
                            for k in range(kc):
                                mm(PS[:, bk, 0:n], wts[t0][si][:, k, mi * 128:(mi + 1) * 128], rhs_fn(k, c0, n), k == 0, k == kc - 1)
                        evac(m, c0, n, *banks)
                        gcount += 1
                        if gcount % adv_every == 0:
                            advance()
                if last_win and after_ct is not None:
                    after_ct(c0, n)

    def wv(Wd):
        return Wd.rearrange("(k p) f -> p k f", p=128)

    def rhs_x(k, c0, n):
        return xTb[:, k, c0:c0 + n]

    def evac_z(m, c0, n, ps_):
        stt(xT[:, m, c0:c0 + n], xT[:, m, c0:c0 + n], ALPHA, ps_, ALU.mult, ALU.add)

    def store_rows(dst, src_fn, nrows, eng_toggle=[0]):
        st_ = ostgn()
        for hh in range(2):
            bk = psn()
            for q in range(4):
                m = hh * 4 + q
                tr(PS[0:nrows, bk, q * 128:(q + 1) * 128], src_fn(m), ident[:])
            if eng_toggle[0] % 2 == 0:
                cp(st_[0:nrows, hh * 512:(hh + 1) * 512], PS[0:nrows, bk, :], eng="act")
            else:
                cp(st_[0:nrows, hh * 512:(hh + 1) * 512], PS[0:nrows, bk, :])
            eng_toggle[0] += 1
        dma("sp", dst, st_[0:nrows, :])

    LANE_B = ARENA_B // 2

    class LV:
        def __init__(self, ap, co):
            self.ap = ap
            self.co = co

        def __getitem__(self, key):
            key = list(key)
            sl = key[-1]
            key[-1] = slice(sl.start - self.co, sl.stop - self.co)
            return self.ap[tuple(key)]

    def lane_body(blk, lane):
        has_s = blk == 1 and lane == 1
        first = blk == 0 and lane == 0
        tok0 = blk * NPB
        co = lane * 512
        WL = 512 + NS
        ctp = [(co, 512)]
        ctiles = ctp + ([(NPB, NS)] if has_s else [])
        abase = lane * LANE_B
        jr = range(lane * 4, lane * 4 + 4)

        def lview(off, shape, dt):
            return LV(aview(abase + off, shape, dt), co)

        def load_p(l):
            st_ = ostgn()
            for q, j in enumerate(jr):
                dma("sp", st_[:, q * 256:(q + 1) * 256], pp[l, tok0 + j * 128: tok0 + (j + 1) * 128, :])
            if has_s:
                dma("sp", psmp[:, :], psm[l, :, :])
            return st_

        def trans_p(st_):
            for kk in range(2):
                bk = psn()
                for q in range(4):
                    tr(PS[:, bk, q * 128:(q + 1) * 128], st_[:, q * 256 + kk * 128:q * 256 + (kk + 1) * 128], ident[:])
                cp(pT[:, kk, co:co + 512], PS[:, bk, :])
                if has_s:
                    bk = psn()
                    tr(PS[:, bk, 0:NS], psmp[0:NS, kk * 128:(kk + 1) * 128], ident[0:NS, 0:NS])
                    cp(pT[:, kk, NPB:NPB + NS], PS[:, bk, 0:NS])

        def load_tokens(src_rows_ap, nrows, col):
            st_ = ostgn()
            dma("sp", st_[0:nrows, :], src_rows_ap)
            for hh in range(2):
                bk = psn()
                for q in range(4):
                    k = hh * 4 + q
                    tr(PS[:, bk, q * 128:q * 128 + nrows], st_[0:nrows, k * 128:(k + 1) * 128], ident[0:nrows, 0:nrows])
                src = PS[:, bk, :].rearrange("p (a b) -> p a b", a=4)[:, :, 0:nrows]
                act(xT[:, hh * 4:(hh + 1) * 4, col:col + nrows], src, AF.Copy)
                cp(xTb[:, hh * 4:(hh + 1) * 4, col:col + nrows], xT[:, hh * 4:(hh + 1) * 4, col:col + nrows])

        for j in jr:
            load_tokens(xp[tok0 + j * 128: tok0 + (j + 1) * 128, :], 128, j * 128)
        if has_s:
            load_tokens(xs[:, :], NS, NPB)
        if nlayers > 0:
            trans_p(load_p(0))
        yield

        for l in range(nlayers):
            kind = l % 4

            def ln1(c0, n, l=l):
                ln(xT, c0, n, "ln1_g", "ln1_b", l * 8, out_f=xT, out_b=xTb)

            if kind == 0:
                vT = lview(0, [8, WL], F32)
                vtm = aview(abase + 8 * WL * 4, [4, 1024], BF16)
                usT = lview(8 * WL * 4 + 8192, [8, WL], BF16)
                W = wv(a_w_in)

                def evac_v(m, c0, n, ps_):
                    act(vT[:, m, c0:c0 + n], ps_, AF.Gelu_apprx_tanh, bias=pcol("a_b_in", 8 + m))
                proj([(W, 1024, 8, rhs_x)], 8, evac_v, ctiles,
                     after_ct=lambda c0, n: ln(vT, c0, n, "a_ln_g", "a_ln_b", 0, out_f=vT))
                yield
                flush()
                for j in jr:
                    last = has_s and j == 7
                    st_ = ostgn() if last else None
                    for hh in range(2):
                        bk = psn()
                        for q in range(4):
                            h = hh * 4 + q
                            tr(PS[:, bk, q * 128:(q + 1) * 128], vT[:, h, j * 128:(j + 1) * 128], ident[:])
                        cp(vtm[:, j - lane * 4, hh * 512:(hh + 1) * 512], PS[:, bk, :])
                        if last:
                            cp(st_[:, hh * 512:(hh + 1) * 512], PS[:, bk, :])
                    if last:
                        dma("sp", cvp[:, :], st_[:, :])
                if has_s:
                    store_rows(cvs[:, :], lambda m: vT[:, m, NPB:NPB + NS], NS)

                def evac_u(m, c0, n, ps_):
                    ut = tmpn()
                    act(ut[:, 0:n], ps_, AF.Gelu_apprx_tanh, bias=pcol("a_b_in", m))
                    if n == 512:
                        bk = psn()
                        for jj in range(4):
                            jl = (c0 - co) // 128 + jj
                            o = PS[:, bk, jj * 128:(jj + 1) * 128]
                            mm(o, ones1[0:1, :], bshi[0:1, m * 128:(m + 1) * 128], True, False)
                            mm(o, ones1[0:1, :], bslo[0:1, m * 128:(m + 1) * 128], False, False)
                            mm(o, vtm[:, jl, m * 128:(m + 1) * 128], wsT[:, m, :], False, True)
                        tt(usT[:, m, c0:c0 + n], ut[:, 0:n], PS[:, bk, 0:n], ALU.mult)
                    else:
                        s_ = small2[:, m, 0:n]
                        ts(s_, vT[:, m, c0:c0 + n], ws00[:, m:m + 1], bs0[:, m:m + 1], ALU.mult, ALU.add)
                        tt(usT[:, m, c0:c0 + n], ut[:, 0:n], s_, ALU.mult)
                proj([(W, 0, 8, rhs_x)], 8, evac_u, ctiles)
                yield
                proj([(wv(a_w_out), 0, 8, lambda k, c0, n: usT[:, k, c0:c0 + n])], 8, evac_z, ctiles, after_ct=ln1)
                yield

            elif kind == 1:
                HLl = 15 + 512
                hext = lview(0, [8, 15 + WL], F32)
                mixT = lview(0, [8, WL], BF16)
                o1 = 8 * (15 + WL) * 4
                tA = aview(abase + o1, [HLl], F32)
                tB = aview(abase + o1 + HLl * 4, [HLl], F32)
                o2 = o1 + 2 * HLl * 4
                poolT = lview(o2, [8, WL], BF16)
                hs = smallf
                if first:
                    memset(hext[:, :, co:co + 15], 0.0)
                else:
                    cp(hext[:, :, co:co + 15], hcar[:, :, :])

                def evac_h(m, c0, n, ps_):
                    if n == 512:
                        act(hext[:, m, 15 + c0:15 + c0 + n], ps_, AF.Copy)
                    else:
                        act(hs[:, m, 0:n], ps_, AF.Copy)
                proj([(wv(b_w_in), 0, 8, rhs_x)], 8, evac_h, ctiles)
                cp(hcar[:, :, :], hext[:, :, co + HLl - 15:co + HLl])
                yield
                if has_s:
                    spt = [ostgn(), ostgn()]
                    for rt in range(2):
                        dma("sp", spt[rt][0:120, :], spool[rt * 120:(rt + 1) * 120, :])
                for m in range(8):
                    g = m // 2
                    w = 2 << g
                    cur = hext[:, m, co:co + HLl]
                    bufs = [tA, tB]
                    sh = 1
                    lo = 1
                    step = 0
                    while sh < w:
                        dst = bufs[step % 2]
                        tt(dst[:, lo:HLl], cur[:, lo:HLl], cur[:, lo - sh:HLl - sh], ALU.add)
                        cur = dst
                        sh *= 2
                        lo = 2 * sh - 1
                        step += 1
                    stt(poolT[:, m, co:co + 512], cur[:, 15:HLl], 1.0 / w, hext[:, m, co + 15:co + HLl], ALU.mult, ALU.subtract)
                    if first:
                        fx = small2[:, m, 0:15]
                        tt(fx, cur[:, 15:30], invc[:, g * 16:g * 16 + 15], ALU.mult)
                        tt(poolT[:, m, 0:15], fx, hext[:, m, 15:30], ALU.subtract)
                    if has_s:
                        bk = psn()
                        for rt in range(2):
                            mm(PS[:, bk, rt * 8:(rt + 1) * 8], spt[rt][0:120, m * 128:(m + 1) * 128],
                               sel[0:120, g * 8:(g + 1) * 8], True, True)
                        s2 = small2[:, m, 0:NS]
                        tt(s2, PS[:, bk, 0:NS], hs[:, m, 0:NS], ALU.add)
                        stt(poolT[:, m, NPB:NPB + NS], s2, 1.0 / w, hs[:, m, 0:NS], ALU.mult, ALU.subtract)
                if has_s:
                    store_rows(plp[:, :], lambda m: hext[:, m, co + HLl - 15:co + HLl], 15)
                    store_rows(pls.rearrange("(s r) d -> s r d", r=15)[:, 14, :], lambda m: hs[:, m, 0:NS], NS)
                    dma("sp", pls.rearrange("(s r) d -> s r d", r=15)[:, 0:14, :],
                        spool.rearrange("(s r) d -> s r d", r=15)[:, 1:15, :])
                yield
                wg = wload(b_w_grp.rearrange("g (kk p) d -> p (g kk) d", p=128))
                for m in range(8):
                    g = m // 2
                    mo = m % 2
                    for (c0, n) in ctiles:
                        bk = psn()
                        for kk in range(2):
                            mm(PS[:, bk, 0:n], wg[:, 2 * g + kk, mo * 128:(mo + 1) * 128], poolT[:, 2 * g + kk, c0:c0 + n],
                               kk == 0, kk == 1)
                        act(mixT[:, m, c0:c0 + n], PS[:, bk, 0:n], AF.Identity, scale=pcol("b_scale", m))
                        advance()
                yield
                proj([(wv(b_w_out), 0, 8, lambda k, c0, n: mixT[:, k, c0:c0 + n])], 8, evac_z, ctiles, after_ct=ln1)
                yield

            elif kind == 2:
                gext = lview(0, [8, 30 + 512], BF16)
                ysT = lview(0, [8, WL], BF16)
                o1 = 8 * (30 + 512) * 2
                Dg = aview(abase + o1, [31, 128], BF16)
                o2 = o1 + 31 * 128 * 2
                yT = lview(o2, [8, WL], F32)
                gs = smallf
                if first:
                    memset(gext[:, :, co:co + 30], 0.0)
                else:
                    cp(gext[:, :, co:co + 30], gcar[:, :, :])
                W = wv(c_w_in)

                def evac_g(m, c0, n, pa, pb):
                    sg = tmpn()
                    act(sg[:, 0:n], pb, AF.Sigmoid, bias=pcol("c_b_in", 8 + m))
                    if n == 512:
                        stt(gext[:, m, 30 + c0:30 + c0 + n], pa, pcol("c_b_in", m), sg[:, 0:n], ALU.add, ALU.mult)
                        if has_s:
                            stt(tails[:, m, 0:30], pa[:, 482:512], pcol("c_b_in", m), sg[:, 482:512], ALU.add, ALU.mult)
                    else:
                        stt(gs[:, m, 0:n], pa, pcol("c_b_in", m), sg[:, 0:n], ALU.add, ALU.mult)
                proj([(W, 0, 8, rhs_x), (W, 1024, 8, rhs_x)], 8, evac_g, ctiles)
                cp(gcar[:, :, :], gext[:, :, co + 512:co + 512 + 30])
                yield
                for m in range(8):
                    for k in range(31):
                        if k % 3 != 2:
                            ts(Dg[:, k, :], identb[:], pcol("c_w_dw", k * 8 + m), None, ALU.mult)
                        else:
                            act(Dg[:, k, :], identb[:], AF.Identity, scale=pcol("c_w_dw", k * 8 + m))
                    for (c0, n) in ctp:
                        bk = psn()
                        for k in range(31):
                            mm(PS[:, bk, 0:n], Dg[:, k, :], gext[:, m, c0 + k:c0 + k + n], k == 0, k == 30)
                        act(yT[:, m, c0:c0 + n], PS[:, bk, 0:n], AF.Identity, bias=pcol("c_b_dw", m))
                        advance()
                    if has_s:
                        cst = ostgn()
                        cst3 = cst[0:120, 0:512].rearrange("p (a b) -> p a b", a=4)
                        dma("sp", cst3, scv.rearrange("(r q) d -> q r d", q=120)[:, :, m * 128:(m + 1) * 128])
                        bk = psn()
                        for rt in range(4):
                            tr(PS[:, bk, rt * 120:(rt + 1) * 120], cst3[:, rt, :], ident[0:120, 0:120])
                        c = PV_OFF["c_w_dw"] + m
                        wtap = bass.AP(P, c, [[PV_ROWS, 128], [0, NS], [8, 30]])
                        prod = tmpn()[:, 0:NS * 30]
                        tt(prod.rearrange("p (s k) -> p s k", k=30), PS[:, bk, 0:NS * 30].rearrange("p (s k) -> p s k", k=30),
                           wtap, ALU.mult)
                        ysum = small2[:, m, 0:NS]
                        pg.add("dve", lambda ysum=ysum, prod=prod: nc.vector.tensor_reduce(
                            out=ysum, in_=prod.rearrange("p (s k) -> p s k", k=30), axis=mybir.AxisListType.X, op=ALU.add),
                            [prod], [ysum])
                        stt(ysum, gs[:, m, 0:NS], pcol("c_w_dw", 30 * 8 + m), ysum, ALU.mult, ALU.add)
                        ts(yT[:, m, NPB:NPB + NS], ysum, pcol("c_b_dw", m), None, ALU.add)
                if has_s:
                    store_rows(cnp[:, :], lambda m: tails[:, m, 0:30], 30)
                    store_rows(cns.rearrange("(s r) d -> s r d", r=30)[:, 29, :], lambda m: gs[:, m, 0:NS], NS)
                    dma("sp", cns.rearrange("(s r) d -> s r d", r=30)[:, 0:29, :],
                        scv.rearrange("(s r) d -> s r d", r=30)[:, 1:30, :])
                for (c0, n) in ctiles:
                    ln(yT, c0, n, "c_ln_g", "c_ln_b", 0, out_b=ysT, func=AF.Silu)
                yield
                proj([(wv(c_w_out), 0, 8, lambda k, c0, n: ysT[:, k, c0:c0 + n])], 8, evac_z, ctiles, after_ct=ln1)
                yield

            else:
                qext = lview(0, [8, 2 + 512], BF16)
                o1 = 8 * (2 + 512) * 2
                tT = lview(o1, [8, WL], BF16)
                o2 = o1 + 8 * WL * 2
                D3 = aview(abase + o2, [3, 128], BF16)
                o3 = o2 + 3 * 128 * 2
                stT = aview(abase + o3, [8, 32], F32)
                qs = smallf
                if first:
                    memset(qext[:, :, co:co + 2], 0.0)
                else:
                    cp(qext[:, :, co:co + 2], qcar[:, :, :])
                if has_s:
                    sst = ostgn()
                    dma("sp", sst[0:32, :], ssc[:, :])
                    for hh in range(2):
                        bk = psn()
                        for q in range(4):
                            tr(PS[:, bk, q * 32:(q + 1) * 32], sst[0:32, (hh * 4 + q) * 128:(hh * 4 + q + 1) * 128], ident[0:32, 0:32])
                        cp(stT[:, hh * 4:(hh + 1) * 4, :], PS[:, bk, 0:128].rearrange("p (a b) -> p a b", a=4))
                W = wv(d_w_in)

                def evac_q(m, c0, n, pc, ph):
                    ct_ = tmpn()
                    act(ct_[:, 0:n], pc, AF.Copy)
                    if n == 512:
                        tt(qext[:, m, 2 + c0:2 + c0 + n], ct_[:, 0:n], ph, ALU.mult)
                        if has_s:
                            tt(tails[:, m, 0:2], ct_[:, 510:512], ph[:, 510:512], ALU.mult)
                    else:
                        tt(qs[:, m, 0:n], ct_[:, 0:n], ph, ALU.mult)
                proj([(W, 1024, 8, rhs_x), (W, 2048, 8, rhs_x)], 8, evac_q, ctiles)
                cp(qcar[:, :, :], qext[:, :, co + 512:co + 512 + 2])
                yield
                if has_s:
                    store_rows(scp[:, :], lambda m: tails[:, m, 0:2], 2)
                    store_rows(scs.rearrange("(s r) d -> s r d", r=2)[:, 1, :], lambda m: qs[:, m, 0:NS], NS)
                    dma("sp", scs.rearrange("(s r) d -> s r d", r=2)[:, 0:1, :],
                        ssc.rearrange("(s r) d -> s r d", r=2)[:, 1:2, :])
                wb = None
                for m in range(8):
                    if m % 4 == 0:
                        wb = wload(W[:, :, m * 128:(m + 4) * 128])
                    for k in range(3):
                        ts(D3[:, k, :], identb[:], pcol("d_w_conv", k * 8 + m), None, ALU.mult)
                    for (c0, n) in ctiles:
                        bkb = psn()
                        for k in range(8):
                            mm(PS[:, bkb, 0:n], wb[:, k, (m % 4) * 128:(m % 4 + 1) * 128], xTb[:, k, c0:c0 + n], k == 0, k == 7)
                        bt = tmpn()
                        act(bt[:, 0:n], PS[:, bkb, 0:n], AF.Copy)
                        if n == 512:
                            bky = psn()
                            for k in range(3):
                                mm(PS[:, bky, 0:n], D3[:, k, :], qext[:, m, c0 + k:c0 + k + n], k == 0, k == 2)
                            tt(tT[:, m, c0:c0 + n], bt[:, 0:n], PS[:, bky, 0:n], ALU.mult)
                        else:
                            y_ = small2[:, m, 0:NS]
                            st3 = stT[:, m, :].rearrange("p (s r) -> p s r", r=2)
                            ts(y_, st3[:, :, 0], pcol("d_w_conv", 0 * 8 + m), None, ALU.mult)
                            stt(y_, st3[:, :, 1], pcol("d_w_conv", 1 * 8 + m), y_, ALU.mult, ALU.add)
                            stt(y_, qs[:, m, 0:NS], pcol("d_w_conv", 2 * 8 + m), y_, ALU.mult, ALU.add)
                            tt(tT[:, m, c0:c0 + n], bt[:, 0:n], y_, ALU.mult)
                        advance()
                yield
                proj([(wv(d_w_out), 0, 8, lambda k, c0, n: tT[:, k, c0:c0 + n])], 8, evac_z, ctiles, after_ct=ln1)
                yield

            if stage < 2:
                flush()
                continue
            hid = lview(0, [32, WL], BF16)

            def evac_hid(m, c0, n, ps_):
                r = tmpn()
                act(r[:, 0:n], ps_, AF.Relu)
                tt(hid[:, m, c0:c0 + n], r[:, 0:n], ps_, ALU.mult)
            proj([(wv(w_up[l]), 0, 8, rhs_x)], 32, evac_hid, ctiles)
            yield
            proj([(wv(w_down[l]), 0, 32, lambda k, c0, n: hid[:, k, c0:c0 + n])], 8, evac_z, ctiles, tile_chunks=1,
                 after_ct=lambda c0, n, l=l: ln(xT, c0, n, "ln2_g", "ln2_b", l * 8, out_f=xT, out_b=xTb))
            yield
            if stage < 3:
                flush()
                continue
            pnext = load_p(l + 1) if l + 1 < nlayers else None

            def evac_ple(m, c0, n, pgate, pproj):
                sg = tmpn()
                act(sg[:, 0:n], pgate, AF.Sigmoid)
                t2 = tmpn()
                tt(t2[:, 0:n], sg[:, 0:n], pproj, ALU.mult)
                tt(xT[:, m, c0:c0 + n], xT[:, m, c0:c0 + n], t2[:, 0:n], ALU.add)
            proj([(wv(w_gate[l]), 0, 8, rhs_x), (wv(w_proj[l]), 0, 2, lambda k, c0, n: pT[:, k, c0:c0 + n])],
                 8, evac_ple, ctiles,
                 after_ct=(lambda c0, n: act(xTb[:, :, c0:c0 + n], xT[:, :, c0:c0 + n], AF.Copy)) if l < nlayers - 1 else None)
            if pnext is not None:
                trans_p(pnext)
            yield

        flush()
        for j in jr:
            store_rows(yp[tok0 + j * 128: tok0 + (j + 1) * 128, :], lambda m, j=j: xT[:, m, j * 128:(j + 1) * 128], 128)
        if has_s:
            store_rows(ys[:, :], lambda m: xT[:, m, NPB:NPB + NS], NS)

    active = [lane_body(0, 0), lane_body(0, 1)]
    nextblk = [1, 1]
    for _ in range(SKEW - 1):
        next(active[0])
    while active[0] is not None or active[1] is not None:
        for i in (0, 1):
            g_ = active[i]
            if g_ is None:
                continue
            try:
                next(g_)
            except StopIteration:
                if nextblk[i] < 2:
                    active[i] = lane_body(nextblk[i], i)
                    nextblk[i] += 1
                else:
                    active[i] = None
    flush()
    return finish()


DBG = set()
_CONST = {}


def _consts():
    if _CONST:
        return _CONST
    ident = np.eye(128, dtype=np.float32)
    ii = np.arange(128)
    mask = (ii[None, :] <= ii[:, None]).astype(np.float32)
    invc = np.zeros((128, 64), np.float32)
    for g, w in enumerate((2, 4, 8, 16)):
        for t in range(16):
            invc[:, g * 16 + t] = 1.0 / min(w, t + 1)
    sel = np.zeros((120, 32), np.float32)
    for g, w in enumerate((2, 4, 8, 16)):
        for r in range(120):
            s, rr = divmod(r, 15)
            if rr >= 15 - (w - 1):
                sel[r, g * 8 + s] = 1.0
    _CONST.update(ident=ident, mask=mask, invc=invc, sel=sel)
    return _CONST


_NC_CACHE = {}


def kernel(x_prompt, x_sample, state_pool, state_conv, state_shortconv, p_prompt, p_sample,
           a_w_in, a_b_in, a_ln_g, a_ln_b, a_w_s, a_b_s, a_w_out,
           b_w_in, b_w_grp, b_scale, b_w_out,
           c_w_in, c_b_in, c_w_dw, c_b_dw, c_ln_g, c_ln_b, c_w_out,
           d_w_in, d_w_conv, d_w_out,
           ln1_g, ln1_b, ln2_g, ln2_b, mlp_w_up, mlp_w_down, ple_w_proj, ple_w_gate, _nlayers=4, _stage=9):
    f = lambda a: np.ascontiguousarray(np.asarray(a, dtype=np.float32))
    vecs = dict(a_b_in=a_b_in, a_ln_g=a_ln_g, a_ln_b=a_ln_b, b_scale=b_scale, c_b_in=c_b_in, c_b_dw=c_b_dw,
                c_ln_g=c_ln_g, c_ln_b=c_ln_b, ln1_g=ln1_g, ln1_b=ln1_b, ln2_g=ln2_g, ln2_b=ln2_b,
                c_w_dw=c_w_dw, d_w_conv=d_w_conv)
    pvec = np.concatenate([f(vecs[n]).reshape(-1, 128) for n, _ in PV_ORDER], axis=0)
    assert pvec.shape == (PV_ROWS, 128)
    cs = _consts()
    shared = dict(pvec=pvec, ident=cs["ident"], mask=cs["mask"], invc=cs["invc"], sel=cs["sel"],
                  a_w_in=f(a_w_in), a_w_s=f(a_w_s), a_b_s=f(a_b_s).reshape(1, D), a_w_out=f(a_w_out),
                  b_w_in=f(b_w_in), b_w_grp=f(b_w_grp), b_w_out=f(b_w_out),
                  c_w_in=f(c_w_in), c_w_out=f(c_w_out), d_w_in=f(d_w_in), d_w_out=f(d_w_out),
                  mlp_w_up=f(mlp_w_up), mlp_w_down=f(mlp_w_down), ple_w_proj=f(ple_w_proj), ple_w_gate=f(ple_w_gate))
    xp_ = f(x_prompt); xs_ = f(x_sample).reshape(128, D)
    pp_ = f(p_prompt); ps_ = f(p_sample).reshape(4, 128, 256)
    sp_ = f(state_pool); sc_ = f(state_conv); ss_ = f(state_shortconv)
    in_maps = []
    for c in range(NCORES):
        sl = slice(c * NS, (c + 1) * NS)
        m = dict(shared)
        m.update(xp=xp_[c], xs=xs_[sl], pp=np.ascontiguousarray(pp_[:, c]), psm=np.ascontiguousarray(ps_[:, sl]),
                 spool=sp_[sl].reshape(NS * 15, D), scv=sc_[sl].reshape(NS * 30, D), ssc=ss_[sl].reshape(NS * 2, D))
        in_maps.append(m)
    if (_nlayers, _stage) not in _NC_CACHE:
        _NC_CACHE[(_nlayers, _stage)] = build(_nlayers, _stage)
    nc = _NC_CACHE[(_nlayers, _stage)]
    res = run_bass_kernel_spmd(nc, in_maps, core_ids=list(range(NCORES)))
    R = res.results
    cat = lambda k: np.stack([np.asarray(r[k], dtype=np.float32) for r in R], axis=0)
    y_prompt = cat("yp")
    y_sample = cat("ys").reshape(128, 1, D)
    chunkv_prompt = cat("cvp")
    chunkv_sample = cat("cvs").reshape(128, 1, D)
    pool_prompt = cat("plp")
    pool_sample = cat("pls").reshape(128, 15, D)
    conv_prompt = cat("cnp")
    conv_sample = cat("cns").reshape(128, 30, D)
    sconv_prompt = cat("scp")
    sconv_sample = cat("scs").reshape(128, 2, D)
    return (y_prompt, y_sample, chunkv_prompt, chunkv_sample, pool_prompt, pool_sample,
            conv_prompt, conv_sample, sconv_prompt, sconv_sample)
```

```python
import numpy as np
import concourse.bass as bass
import concourse.mybir as mybir
from concourse.bass_utils import run_bass_kernel_spmd

F32 = mybir.dt.float32
BF16 = mybir.dt.bfloat16
AF = mybir.ActivationFunctionType
ALU = mybir.AluOpType

NCORES = 8
D = 1024
SEQ = 2048
NS = 16
NPB = 1024
NCM = NPB + NS
ALPHA = float(8 ** 0.25)
EPS = 1e-5
BLKB = 64
NSLOT = 5
SKEW = 2
SLOT_ELEMS = 4096

PV_ORDER = [("a_b_in", 16), ("a_ln_g", 8), ("a_ln_b", 8), ("b_scale", 8), ("c_b_in", 16), ("c_b_dw", 8),
            ("c_ln_g", 8), ("c_ln_b", 8), ("ln1_g", 32), ("ln1_b", 32), ("ln2_g", 32), ("ln2_b", 32),
            ("c_w_dw", 248), ("d_w_conv", 24)]
PV_OFF = {}
_o = 0
for _n, _c in PV_ORDER:
    PV_OFF[_n] = _o
    _o += _c
PV_ROWS = _o


class Prog:
    def __init__(self, nc):
        self.nc = nc
        self.ops = []
        self.st = {}
        self.eng = {"pe": nc.tensor, "act": nc.scalar, "dve": nc.vector, "pool": nc.gpsimd, "sp": nc.sync}

    def blocks(self, ap):
        t = ap.tensor
        if type(t).__name__.startswith("DRam"):
            return None
        es = mybir.dt.size(ap.dtype)
        dims = ap.ap
        pstride = dims[0][0]
        off = (ap.offset % pstride) * es
        fd = list(dims[1:])
        starts = np.zeros(1, dtype=np.int64)
        if len(fd) == 0:
            run = es
        else:
            if fd[-1][0] == 1:
                run = fd[-1][1] * es
                outer = fd[:-1]
            else:
                run = es
                outer = fd
            for step, cnt in outer:
                starts = (starts[:, None] + (np.arange(cnt, dtype=np.int64) * step * es)[None, :]).reshape(-1)
        name = t.name
        blkb = 2048 if name == "PS" else BLKB
        lo = (off + starts) // blkb
        hi = (off + starts + run - 1) // blkb
        out = set()
        for l, h in zip(lo.tolist(), hi.tolist()):
            for b in range(l, h + 1):
                out.add((name, b))
        return out

    def add(self, issuer, fn, reads=(), writes=(), dma=False):
        opid = len(self.ops)
        raw = set()
        oth = set()
        rb = set()
        wb = set()
        for ap in reads:
            b = self.blocks(ap)
            if b:
                rb |= b
        for ap in writes:
            b = self.blocks(ap)
            if b:
                wb |= b
        st = self.st
        for k in rb:
            s = st.get(k)
            if s is not None and s[0] is not None:
                raw.add(s[0])
            if s is not None and k[0] == "PS":
                for rk_, rop in s[1].items():
                    if rk_ != issuer:
                        raw.add(rop)
        for k in wb:
            s = st.get(k)
            if s is not None:
                if s[0] is not None:
                    oth.add(s[0])
                oth.update(s[1].values())
        rk = ("dma", opid) if dma else issuer
        for k in rb:
            s = st.get(k)
            if s is None:
                s = st[k] = [None, {}]
            s[1][rk] = opid
        for k in wb:
            st[k] = [opid, {}]
        deps = set()
        for p in raw | oth:
            if p == opid:
                continue
            po = self.ops[p]
            if (not dma) and (not po["dma"]) and po["issuer"] == issuer:
                if issuer == "pe":
                    continue
            deps.add(p)
        self.ops.append(dict(issuer=issuer, fn=fn, deps=deps, dma=dma))
        return opid

    def emit(self):
        nc = self.nc
        ops = self.ops
        needs = [False] * len(ops)
        for o in ops:
            for p in o["deps"]:
                needs[p] = True
        esem = {k: nc.semaphore("s_" + k).__enter__() for k in ("pe", "act", "dve", "pool")}
        ecnt = {k: 0 for k in esem}
        NSP = 8
        dsem = {"sp": [nc.semaphore("d_sp%d" % i).__enter__() for i in range(NSP)],
                "pool": [nc.semaphore("d_pl%d" % i).__enter__() for i in range(8)]}
        dtot = {"sp": [0] * NSP, "pool": [0] * 8}
        dnext = {"sp": 0, "pool": 0}
        waited = {k: {} for k in self.eng}
        inflight = {"sp": [], "pool": []}
        MAXQ = {"sp": 6, "pool": 4}
        events = [None] * len(ops)

        def wait(issuer, sem, val):
            w = waited[issuer]
            key = id(sem)
            if w.get(key, 0) >= val:
                return
            w[key] = val
            self.eng[issuer].wait_ge(sem, val)

        for i, o in enumerate(ops):
            iss = o["issuer"]
            for p in sorted(o["deps"]):
                sem, val = events[p]
                wait(iss, sem, val)
            if o["dma"]:
                fl = inflight[iss]
                if len(fl) >= MAXQ[iss]:
                    wait(iss, *fl[-MAXQ[iss]])
                j = dnext[iss]
                dnext[iss] = (j + 1) % len(dsem[iss])
                sem = dsem[iss][j]
                if dtot[iss][j] > 0:
                    wait(iss, sem, dtot[iss][j])
                ins = o["fn"]()
                dtot[iss][j] += 16
                ins.then_inc(sem, 16)
                events[i] = (sem, dtot[iss][j])
                fl.append(events[i])
            else:
                ins = o["fn"]()
                if needs[i]:
                    ecnt[iss] += 1
                    ins.then_inc(esem[iss], 1)
                    events[i] = (esem[iss], ecnt[iss])
        for q in ("sp", "pool"):
            for j, sem in enumerate(dsem[q]):
                if dtot[q][j] > 0:
                    nc.sync.wait_ge(sem, dtot[q][j])


def build(nlayers=4, stage=9):
    nc = bass.Bass("TRN2", target_bir_lowering=False)
    pg = Prog(nc)

    def din(name, shape):
        return nc.dram_tensor(name, list(shape), F32, kind="ExternalInput").ap()

    def dout(name, shape):
        return nc.dram_tensor(name, list(shape), F32, kind="ExternalOutput").ap()

    def finish():
        with nc.allow_non_contiguous_dma(reason="small strided parameter/state loads"):
            pg.emit()
        return nc

    xp = din("xp", [SEQ, D]); xs = din("xs", [NS, D])
    pp = din("pp", [4, SEQ, 256]); psm = din("psm", [4, NS, 256])
    spool = din("spool", [NS * 15, D]); scv = din("scv", [NS * 30, D]); ssc = din("ssc", [NS * 2, D])
    pvec = din("pvec", [PV_ROWS, 128])
    identd = din("ident", [128, 128]); maskd = din("mask", [128, 128])
    invcd = din("invc", [128, 64]); seld = din("sel", [120, 32])
    a_w_in = din("a_w_in", [D, 2 * D]); a_w_s = din("a_w_s", [8, 128, 128]); a_b_s = din("a_b_s", [1, D])
    a_w_out = din("a_w_out", [D, D])
    b_w_in = din("b_w_in", [D, D]); b_w_grp = din("b_w_grp", [4, 256, 256]); b_w_out = din("b_w_out", [D, D])
    c_w_in = din("c_w_in", [D, 2 * D]); c_w_out = din("c_w_out", [D, D])
    d_w_in = din("d_w_in", [D, 3 * D]); d_w_out = din("d_w_out", [D, D])
    w_up = din("mlp_w_up", [4, D, 4 * D]); w_down = din("mlp_w_down", [4, 4 * D, D])
    w_proj = din("ple_w_proj", [4, 256, D]); w_gate = din("ple_w_gate", [4, D, D])

    yp = dout("yp", [SEQ, D]); ys = dout("ys", [NS, D])
    cvp = dout("cvp", [128, D]); cvs = dout("cvs", [NS, D])
    plp = dout("plp", [15, D]); pls = dout("pls", [NS * 15, D])
    cnp = dout("cnp", [30, D]); cns = dout("cns", [NS * 30, D])
    scp = dout("scp", [2, D]); scs = dout("scs", [NS * 2, D])

    def sb(name, shape, dt=F32):
        return nc.sbuf_tensor(name, list(shape), dt).__enter__()

    xT = sb("xT", [128, 8, NCM]); xTb = sb("xTb", [128, 8, NCM], BF16)
    ring = sb("ring", [128, NSLOT, SLOT_ELEMS], BF16)
    ARENA_B = 67584
    arena = sb("arena", [128, ARENA_B // 4])
    psmp = sb("psmp", [NS, 256]); pT = sb("pT", [128, 2, NCM], BF16)
    zb = sb("zb", [128, 8, 512], BF16); zsq = sb("zsq", [128, 8, 512], BF16)
    msq = sb("msq", [128, 512]); rstd = sb("rstd", [128, 512]); var = msq; sd = msq
    P = sb("P", [128, PV_ROWS])
    ident = sb("identf", [128, 128]); identb = sb("identb", [128, 128], BF16)
    onesb = sb("onesb", [128, 128], BF16); ones1 = sb("ones1", [1, 128], BF16)
    wsT = sb("wsT", [128, 8, 128], BF16)
    bshi = sb("bshi", [1, D], BF16); bslo = sb("bslo", [1, D], BF16)
    ws00 = sb("ws00", [128, 8]); bs0 = sb("bs0", [128, 8])
    invc = sb("invct", [128, 64]); sel = sb("selt", [120, 32])
    epsc = sb("epsc", [128, 1])
    hcar = sb("hcar", [128, 8, 15]); gcar = sb("gcar", [128, 8, 30], BF16); qcar = sb("qcar", [128, 8, 2], BF16)
    tmpf = [sb("tmpf%d" % i, [128, 512]) for i in range(3)]
    ostg = [sb("ostg%d" % i, [128, D]) for i in range(2)]
    smallf = sb("smallf", [128, 8, 32])
    small2 = sb("small2", [128, 8, 32])
    tails = sb("tails", [128, 8, 32])
    PS = nc.psum_tensor("PS", [128, 8, 512], F32).__enter__()

    def aview(byte_off, shape, dt):
        es = mybir.dt.size(dt)
        n = int(np.prod(shape))
        a = arena[:, byte_off // 4: byte_off // 4 + (n * es + 3) // 4]
        if dt != F32:
            a = a.bitcast(dt)
        a = a[:, 0:n]
        if len(shape) == 1:
            return a
        if len(shape) == 2:
            return a.rearrange("p (a b) -> p a b", a=shape[0])
        return a.rearrange("p (a b c) -> p a b c", a=shape[0], b=shape[1])

    E = pg.eng
    cnt = {"ps": 0, "tmp": 0, "ostg": 0, "ws": 0}

    def psn():
        b = cnt["ps"] % 6
        cnt["ps"] += 1
        return b

    def tmpn():
        t = tmpf[cnt["tmp"] % 3]
        cnt["tmp"] += 1
        return t

    def ostgn():
        t = ostg[cnt["ostg"] % 2]
        cnt["ostg"] += 1
        return t

    def isap(x):
        return not isinstance(x, (int, float)) and x is not None

    def act(out, in_, func, bias=None, scale=None):
        kw = {}
        rd = [in_]
        if bias is not None:
            kw["bias"] = bias
            if isap(bias):
                rd.append(bias)
        if scale is not None:
            kw["scale"] = scale
            if isap(scale):
                rd.append(scale)
        pg.add("act", lambda: nc.scalar.activation(out=out, in_=in_, func=func, **kw), rd, [out])

    def tt(out, in0, in1, op, eng="dve"):
        pg.add(eng, lambda: E[eng].tensor_tensor(out=out, in0=in0, in1=in1, op=op), [in0, in1], [out])

    def stt(out, in0, scalar, in1, op0, op1):
        rd = [in0, in1] + ([scalar] if isap(scalar) else [])
        pg.add("dve", lambda: nc.vector.scalar_tensor_tensor(out=out, in0=in0, scalar=scalar, in1=in1, op0=op0, op1=op1),
               rd, [out])

    def ts(out, in0, s1, s2, op0, op1=None, eng="dve"):
        rd = [in0] + [s for s in (s1, s2) if isap(s)]
        if op1 is None:
            pg.add(eng, lambda: E[eng].tensor_scalar(out=out, in0=in0, scalar1=s1, scalar2=None, op0=op0), rd, [out])
        else:
            pg.add(eng, lambda: E[eng].tensor_scalar(out=out, in0=in0, scalar1=s1, scalar2=s2, op0=op0, op1=op1), rd, [out])

    def cp(out, in_, eng="dve"):
        if eng == "act":
            act(out, in_, AF.Copy)
        else:
            pg.add(eng, lambda: E[eng].tensor_copy(out=out, in_=in_), [in_], [out])

    def memset(ap, v, eng="dve"):
        pg.add(eng, lambda: E[eng].memset(ap, v), [], [ap])

    def mm(out, lhsT, rhs, start, stop):
        pg.add("pe", lambda: nc.tensor.matmul(out, lhsT=lhsT, rhs=rhs, start=start, stop=stop), [lhsT, rhs], [out])

    def tr(out, in_, idn):
        pg.add("pe", lambda: nc.tensor.transpose(out=out, in_=in_, identity=idn), [in_, idn], [out])

    def dma(q, out, in_):
        pg.add(q, lambda: E[q].dma_start(out=out, in_=in_), [in_], [out], dma=True)

    def wload(dview):
        s = cnt["ws"] % NSLOT
        cnt["ws"] += 1
        kc, ncols = dview.shape[1], dview.shape[2]
        assert kc * ncols <= SLOT_ELEMS
        dst = ring[:, s, 0:kc * ncols].rearrange("p (k f) -> p k f", k=kc)
        dma("pool", dst, dview)
        return dst

    def pcol(name, idx):
        c = PV_OFF[name] + idx
        return P[:, c:c + 1]

    dma("sp", ident[:], identd)
    mask = aview(16384, [128], F32)
    dma("sp", mask[:, :], maskd)
    dma("sp", invc[:], invcd)
    dma("sp", sel[:], seld)
    memset(onesb[:], 1.0 / 1024.0)
    memset(ones1[:], 1.0)
    memset(epsc[:], EPS)
    cp(identb[:], ident[:])
    if "nosetup" in DBG:
        return finish()
    pst = aview(0, [4, 128], F32)
    for r in range(4):
        dma("sp", pst[0:120, r, :], pvec[r * 120:(r + 1) * 120, :])
    b = psn()
    for r in range(4):
        tr(PS[:, b, r * 120:(r + 1) * 120], pst[0:120, r, :], ident[0:120, 0:120])
    cp(P[:], PS[:, b, 0:PV_ROWS])
    if "nows" in DBG:
        return finish()
    wsl = aview(4096, [8, 128], F32)
    with nc.allow_non_contiguous_dma(reason="small param load"):
        dma("sp", wsl, a_w_s.rearrange("h i j -> i h j"))
        dma("sp", ws00[:], bass.AP(a_w_s.tensor, 0, [[0, 128], [128 * 128, 8]]))
        dma("sp", bs0[:], bass.AP(a_b_s.tensor, 0, [[0, 128], [128, 8]]))
    for h in range(8):
        tt(wsl[:, h, :], wsl[:, h, :], mask[:, :], ALU.mult)
    for hh in range(2):
        b = psn()
        for q in range(4):
            tr(PS[:, b, q * 128:(q + 1) * 128], wsl[:, hh * 4 + q, :], ident[:])
        cp(wsT[:, hh * 4:(hh + 1) * 4, :], PS[:, b, :].rearrange("p (a b) -> p a b", a=4))
    bsf = aview(8192, [D], F32); bsr = aview(12288, [D], F32)
    dma("sp", bsf[0:1, :], a_b_s)
    cp(bshi[:], bsf[0:1, :])
    tt(bsr[0:1, :], bsf[0:1, :], bshi[:], ALU.subtract)
    cp(bslo[:], bsr[0:1, :])

    if "noblocks" in DBG:
        return finish()
    pending = []

    def advance():
        if pending:
            item = pending[0]
            try:
                next(item[1])
            except StopIteration:
                pending.pop(0)

    def flush(c0=None):
        if c0 is None:
            idx = len(pending) - 1
        else:
            idx = -1
            for i, item in enumerate(pending):
                if item[0] == c0:
                    idx = i
        for _ in range(idx + 1):
            item = pending.pop(0)
            for _ in item[1]:
                pass

    def ln_gen(zT, c0, n, gname, bname, pidx, out_f, out_b, func):
        for hh in range(2):
            zv = zT[:, hh * 4:(hh + 1) * 4, c0:c0 + n]
            act(zb[:, hh * 4:(hh + 1) * 4, 0:n], zv, AF.Copy)
            act(zsq[:, hh * 4:(hh + 1) * 4, 0:n], zv, AF.Square)
        yield
        bm = 6
        be = 7
        for k in range(8):
            mm(PS[:, bm, 0:n], onesb[:], zb[:, k, 0:n], k == 0, k == 7)
        for k in range(8):
            mm(PS[:, be, 0:n], onesb[:], zsq[:, k, 0:n], k == 0, k == 7)
        yield
        act(msq[:, 0:n], PS[:, bm, 0:n], AF.Square)
        tt(var[:, 0:n], PS[:, be, 0:n], msq[:, 0:n], ALU.subtract)
        act(sd[:, 0:n], var[:, 0:n], AF.Ln, bias=epsc[:, 0:1])
        act(rstd[:, 0:n], sd[:, 0:n], AF.Exp, scale=-0.5)
        stt(PS[:, bm, 0:n], PS[:, bm, 0:n], -1.0, rstd[:, 0:n], ALU.mult, ALU.mult)
        act(PS[:, be, 0:n], rstd[:, 0:n], AF.Copy)
        yield
        NPOOL = 0
        for k in range(8):
            zk = zT[:, k, c0:c0 + n]
            if k >= 8 - NPOOL:
                tt(zk, zk, rstd[:, 0:n], ALU.mult, eng="pool")
                tt(zk, zk, nmrs[:, 0:n], ALU.add, eng="pool")
            else:
                tt(zk, zk, PS[:, be, 0:n], ALU.mult)
                tt(zk, zk, PS[:, bm, 0:n], ALU.add)
            if k % 2 == 1 and k < 7:
                yield
        yield
        order = [5, 6, 7, 0, 1, 2, 3, 4] if NPOOL else list(range(8))
        for i, k in enumerate(order):
            zk = zT[:, k, c0:c0 + n]
            g = pcol(gname, pidx + k); bb = pcol(bname, pidx + k)
            if func == AF.Identity:
                if out_b is not None:
                    ts(out_b[:, k, c0:c0 + n], zk, g, bb, ALU.mult, ALU.add)
                if out_f is not None:
                    act(out_f[:, k, c0:c0 + n], zk, AF.Identity, bias=bb, scale=g)
            else:
                act(out_b[:, k, c0:c0 + n], zk, func, bias=bb, scale=g)
            if i % 2 == 1 and i < 7:
                yield

    def ln(zT, c0, n, gname, bname, pidx, out_f=None, out_b=None, func=AF.Identity):
        pending.append((c0, ln_gen(zT, c0, n, gname, bname, pidx, out_f, out_b, func)))

    def proj(specs, nchunks, evac, ctiles, tile_chunks=4, after_ct=None, win=None):
        if win is None:
            win = 1
        ngroups = nchunks * len(ctiles)
        adv_every = max(1, ngroups // 10)
        gcount = 0
        steps = list(range(0, nchunks, tile_chunks))
        for w0 in range(0, len(steps), win):
            wsteps = steps[w0:w0 + win]
            last_win = w0 + win >= len(steps)
            wts = {}
            for t0 in wsteps:
                tcn = min(tile_chunks, nchunks - t0)
                wts[t0] = [wload(Wv[:, :, col0 + t0 * 128: col0 + (t0 + tcn) * 128]) for (Wv, col0, kc, rhs_fn) in specs]
            for (c0, n) in ctiles:
                flush(c0)
                for t0 in wsteps:
                    tcn = min(tile_chunks, nchunks - t0)
                    for mi in range(tcn):
                        m = t0 + mi
                        banks = []
                        for si, (Wv, col0, kc, rhs_fn) in enumerate(specs):
                            bk = psn()
                            banks.append(PS[:, bk, 0:n])
                            for k in range(kc):
                                mm(PS[:, bk, 0:n], wts[t0][si][:, k, mi * 128:(mi + 1) * 128], rhs_fn(k, c0, n), k == 0, k == kc - 1)
                        evac(m, c0, n, *banks)
                        gcount += 1
                        if gcount % adv_every == 0:
                            advance()
                            if specs[0][2] >= 32:
                                advance()
                if last_win and after_ct is not None:
                    after_ct(c0, n)

    def wv(Wd):
        return Wd.rearrange("(k p) f -> p k f", p=128)

    def rhs_x(k, c0, n):
        return xTb[:, k, c0:c0 + n]

    def evac_z(m, c0, n, ps_):
        stt(xT[:, m, c0:c0 + n], xT[:, m, c0:c0 + n], ALPHA, ps_, ALU.mult, ALU.add)

    def store_rows(dst, src_fn, nrows, eng_toggle=[0]):
        st_ = ostgn()
        for hh in range(2):
            bk = psn()
            for q in range(4):
                m = hh * 4 + q
                tr(PS[0:nrows, bk, q * 128:(q + 1) * 128], src_fn(m), ident[:])
            if eng_toggle[0] % 2 == 0:
                cp(st_[0:nrows, hh * 512:(hh + 1) * 512], PS[0:nrows, bk, :], eng="act")
            else:
                cp(st_[0:nrows, hh * 512:(hh + 1) * 512], PS[0:nrows, bk, :])
            eng_toggle[0] += 1
        dma("sp", dst, st_[0:nrows, :])

    LANE_B = ARENA_B // 2

    class LV:
        def __init__(self, ap, co):
            self.ap = ap
            self.co = co

        def __getitem__(self, key):
            key = list(key)
            sl = key[-1]
            key[-1] = slice(sl.start - self.co, sl.stop - self.co)
            return self.ap[tuple(key)]

    def lane_body(blk, lane):
        has_s = blk == 1 and lane == 1
        first = blk == 0 and lane == 0
        tok0 = blk * NPB
        co = lane * 512
        WL = 512 + NS
        ctp = [(co, 512)]
        ctiles = ctp + ([(NPB, NS)] if has_s else [])
        abase = lane * LANE_B
        jr = range(lane * 4, lane * 4 + 4)

        def lview(off, shape, dt):
            return LV(aview(abase + off, shape, dt), co)

        def load_p(l):
            st_ = ostgn()
            for q, j in enumerate(jr):
                dma("sp", st_[:, q * 256:(q + 1) * 256], pp[l, tok0 + j * 128: tok0 + (j + 1) * 128, :])
            if has_s:
                dma("sp", psmp[:, :], psm[l, :, :])
            return st_

        def trans_p(st_):
            for kk in range(2):
                bk = psn()
                for q in range(4):
                    tr(PS[:, bk, q * 128:(q + 1) * 128], st_[:, q * 256 + kk * 128:q * 256 + (kk + 1) * 128], ident[:])
                cp(pT[:, kk, co:co + 512], PS[:, bk, :])
                if has_s:
                    bk = psn()
                    tr(PS[:, bk, 0:NS], psmp[0:NS, kk * 128:(kk + 1) * 128], ident[0:NS, 0:NS])
                    cp(pT[:, kk, NPB:NPB + NS], PS[:, bk, 0:NS])

        def load_tokens(src_rows_ap, nrows, col):
            st_ = ostgn()
            dma("sp", st_[0:nrows, :], src_rows_ap)
            for hh in range(2):
                bk = psn()
                for q in range(4):
                    k = hh * 4 + q
                    tr(PS[:, bk, q * 128:q * 128 + nrows], st_[0:nrows, k * 128:(k + 1) * 128], ident[0:nrows, 0:nrows])
                src = PS[:, bk, :].rearrange("p (a b) -> p a b", a=4)[:, :, 0:nrows]
                act(xT[:, hh * 4:(hh + 1) * 4, col:col + nrows], src, AF.Copy)
                cp(xTb[:, hh * 4:(hh + 1) * 4, col:col + nrows], xT[:, hh * 4:(hh + 1) * 4, col:col + nrows])

        for j in jr:
            load_tokens(xp[tok0 + j * 128: tok0 + (j + 1) * 128, :], 128, j * 128)
        if has_s:
            load_tokens(xs[:, :], NS, NPB)
        if nlayers > 0:
            trans_p(load_p(0))
        yield

        for l in range(nlayers):
            kind = l % 4

            def ln1(c0, n, l=l):
                ln(xT, c0, n, "ln1_g", "ln1_b", l * 8, out_f=xT, out_b=xTb)

            if kind == 0:
                vT = lview(0, [8, WL], F32)
                vtm = aview(abase + 8 * WL * 4, [4, 1024], BF16)
                usT = lview(8 * WL * 4 + 8192, [8, WL], BF16)
                W = wv(a_w_in)

                def evac_v(m, c0, n, ps_):
                    act(vT[:, m, c0:c0 + n], ps_, AF.Gelu_apprx_tanh, bias=pcol("a_b_in", 8 + m))
                proj([(W, 1024, 8, rhs_x)], 8, evac_v, ctiles,
                     after_ct=lambda c0, n: ln(vT, c0, n, "a_ln_g", "a_ln_b", 0, out_f=vT))
                yield
                flush()
                for j in jr:
                    last = has_s and j == 7
                    st_ = ostgn() if last else None
                    for hh in range(2):
                        bk = psn()
                        for q in range(4):
                            h = hh * 4 + q
                            tr(PS[:, bk, q * 128:(q + 1) * 128], vT[:, h, j * 128:(j + 1) * 128], ident[:])
                        cp(vtm[:, j - lane * 4, hh * 512:(hh + 1) * 512], PS[:, bk, :])
                        if last:
                            cp(st_[:, hh * 512:(hh + 1) * 512], PS[:, bk, :])
                    if last:
                        dma("sp", cvp[:, :], st_[:, :])
                if has_s:
                    store_rows(cvs[:, :], lambda m: vT[:, m, NPB:NPB + NS], NS)

                def evac_u(m, c0, n, ps_):
                    ut = tmpn()
                    act(ut[:, 0:n], ps_, AF.Gelu_apprx_tanh, bias=pcol("a_b_in", m))
                    if n == 512:
                        bk = psn()
                        for jj in range(4):
                            jl = (c0 - co) // 128 + jj
                            o = PS[:, bk, jj * 128:(jj + 1) * 128]
                            mm(o, ones1[0:1, :], bshi[0:1, m * 128:(m + 1) * 128], True, False)
                            mm(o, ones1[0:1, :], bslo[0:1, m * 128:(m + 1) * 128], False, False)
                            mm(o, vtm[:, jl, m * 128:(m + 1) * 128], wsT[:, m, :], False, True)
                        tt(usT[:, m, c0:c0 + n], ut[:, 0:n], PS[:, bk, 0:n], ALU.mult)
                    else:
                        s_ = small2[:, m, 0:n]
                        ts(s_, vT[:, m, c0:c0 + n], ws00[:, m:m + 1], bs0[:, m:m + 1], ALU.mult, ALU.add)
                        tt(usT[:, m, c0:c0 + n], ut[:, 0:n], s_, ALU.mult)
                proj([(W, 0, 8, rhs_x)], 8, evac_u, ctiles)
                yield
                proj([(wv(a_w_out), 0, 8, lambda k, c0, n: usT[:, k, c0:c0 + n])], 8, evac_z, ctiles, after_ct=ln1)
                yield

            elif kind == 1:
                HLl = 15 + 512
                hext = lview(0, [8, 15 + WL], F32)
                mixT = lview(0, [8, WL], BF16)
                o1 = 8 * (15 + WL) * 4
                tA = aview(abase + o1, [HLl], F32)
                tB = aview(abase + o1 + HLl * 4, [HLl], F32)
                o2 = o1 + 2 * HLl * 4
                poolT = lview(o2, [8, WL], BF16)
                hs = smallf
                if first:
                    memset(hext[:, :, co:co + 15], 0.0)
                else:
                    cp(hext[:, :, co:co + 15], hcar[:, :, :])

                def evac_h(m, c0, n, ps_):
                    if n == 512:
                        act(hext[:, m, 15 + c0:15 + c0 + n], ps_, AF.Copy)
                    else:
                        act(hs[:, m, 0:n], ps_, AF.Copy)
                proj([(wv(b_w_in), 0, 8, rhs_x)], 8, evac_h, ctiles)
                cp(hcar[:, :, :], hext[:, :, co + HLl - 15:co + HLl])
                yield
                if has_s:
                    spt = [ostgn(), ostgn()]
                    for rt in range(2):
                        dma("sp", spt[rt][0:120, :], spool[rt * 120:(rt + 1) * 120, :])
                for m in range(8):
                    g = m // 2
                    w = 2 << g
                    cur = hext[:, m, co:co + HLl]
                    bufs = [tA, tB]
                    sh = 1
                    lo = 1
                    step = 0
                    while sh < w:
                        dst = bufs[step % 2]
                        tt(dst[:, lo:HLl], cur[:, lo:HLl], cur[:, lo - sh:HLl - sh], ALU.add)
                        cur = dst
                        sh *= 2
                        lo = 2 * sh - 1
                        step += 1
                    stt(poolT[:, m, co:co + 512], cur[:, 15:HLl], 1.0 / w, hext[:, m, co + 15:co + HLl], ALU.mult, ALU.subtract)
                    if first:
                        fx = small2[:, m, 0:15]
                        tt(fx, cur[:, 15:30], invc[:, g * 16:g * 16 + 15], ALU.mult)
                        tt(poolT[:, m, 0:15], fx, hext[:, m, 15:30], ALU.subtract)
                    if has_s:
                        bk = psn()
                        for rt in range(2):
                            mm(PS[:, bk, rt * 8:(rt + 1) * 8], spt[rt][0:120, m * 128:(m + 1) * 128],
                               sel[0:120, g * 8:(g + 1) * 8], True, True)
                        s2 = small2[:, m, 0:NS]
                        tt(s2, PS[:, bk, 0:NS], hs[:, m, 0:NS], ALU.add)
                        stt(poolT[:, m, NPB:NPB + NS], s2, 1.0 / w, hs[:, m, 0:NS], ALU.mult, ALU.subtract)
                if has_s:
                    store_rows(plp[:, :], lambda m: hext[:, m, co + HLl - 15:co + HLl], 15)
                    store_rows(pls.rearrange("(s r) d -> s r d", r=15)[:, 14, :], lambda m: hs[:, m, 0:NS], NS)
                    dma("sp", pls.rearrange("(s r) d -> s r d", r=15)[:, 0:14, :],
                        spool.rearrange("(s r) d -> s r d", r=15)[:, 1:15, :])
                yield
                wg = wload(b_w_grp.rearrange("g (kk p) d -> p (g kk) d", p=128))
                for m in range(8):
                    g = m // 2
                    mo = m % 2
                    for (c0, n) in ctiles:
                        bk = psn()
                        for kk in range(2):
                            mm(PS[:, bk, 0:n], wg[:, 2 * g + kk, mo * 128:(mo + 1) * 128], poolT[:, 2 * g + kk, c0:c0 + n],
                               kk == 0, kk == 1)
                        act(mixT[:, m, c0:c0 + n], PS[:, bk, 0:n], AF.Identity, scale=pcol("b_scale", m))
                        advance()
                yield
                proj([(wv(b_w_out), 0, 8, lambda k, c0, n: mixT[:, k, c0:c0 + n])], 8, evac_z, ctiles, after_ct=ln1)
                yield

            elif kind == 2:
                gext = lview(0, [8, 30 + 512], BF16)
                ysT = lview(0, [8, WL], BF16)
                o1 = 8 * (30 + 512) * 2
                Dg = aview(abase + o1, [31, 128], BF16)
                o2 = o1 + 31 * 128 * 2
                yT = lview(o2, [8, WL], F32)
                gs = smallf
                if first:
                    memset(gext[:, :, co:co + 30], 0.0)
                else:
                    cp(gext[:, :, co:co + 30], gcar[:, :, :])
                W = wv(c_w_in)

                def evac_g(m, c0, n, pa, pb):
                    sg = tmpn()
                    act(sg[:, 0:n], pb, AF.Sigmoid, bias=pcol("c_b_in", 8 + m))
                    if n == 512:
                        stt(gext[:, m, 30 + c0:30 + c0 + n], pa, pcol("c_b_in", m), sg[:, 0:n], ALU.add, ALU.mult)
                        if has_s:
                            stt(tails[:, m, 0:30], pa[:, 482:512], pcol("c_b_in", m), sg[:, 482:512], ALU.add, ALU.mult)
                    else:
                        stt(gs[:, m, 0:n], pa, pcol("c_b_in", m), sg[:, 0:n], ALU.add, ALU.mult)
                proj([(W, 0, 8, rhs_x), (W, 1024, 8, rhs_x)], 8, evac_g, ctiles)
                cp(gcar[:, :, :], gext[:, :, co + 512:co + 512 + 30])
                yield
                for m in range(8):
                    for k in range(31):
                        ts(Dg[:, k, :], identb[:], pcol("c_w_dw", k * 8 + m), None, ALU.mult)
                    for (c0, n) in ctp:
                        bk = psn()
                        for k in range(31):
                            mm(PS[:, bk, 0:n], Dg[:, k, :], gext[:, m, c0 + k:c0 + k + n], k == 0, k == 30)
                        act(yT[:, m, c0:c0 + n], PS[:, bk, 0:n], AF.Identity, bias=pcol("c_b_dw", m))
                        advance()
                    if has_s:
                        cst = ostgn()
                        cst3 = cst[0:120, 0:512].rearrange("p (a b) -> p a b", a=4)
                        dma("sp", cst3, scv.rearrange("(r q) d -> q r d", q=120)[:, :, m * 128:(m + 1) * 128])
                        bk = psn()
                        for rt in range(4):
                            tr(PS[:, bk, rt * 120:(rt + 1) * 120], cst3[:, rt, :], ident[0:120, 0:120])
                        c = PV_OFF["c_w_dw"] + m
                        wtap = bass.AP(P, c, [[PV_ROWS, 128], [0, NS], [8, 30]])
                        prod = tmpn()[:, 0:NS * 30]
                        tt(prod.rearrange("p (s k) -> p s k", k=30), PS[:, bk, 0:NS * 30].rearrange("p (s k) -> p s k", k=30),
                           wtap, ALU.mult)
                        ysum = small2[:, m, 0:NS]
                        pg.add("dve", lambda ysum=ysum, prod=prod: nc.vector.tensor_reduce(
                            out=ysum, in_=prod.rearrange("p (s k) -> p s k", k=30), axis=mybir.AxisListType.X, op=ALU.add),
                            [prod], [ysum])
                        stt(ysum, gs[:, m, 0:NS], pcol("c_w_dw", 30 * 8 + m), ysum, ALU.mult, ALU.add)
                        ts(yT[:, m, NPB:NPB + NS], ysum, pcol("c_b_dw", m), None, ALU.add)
                if has_s:
                    store_rows(cnp[:, :], lambda m: tails[:, m, 0:30], 30)
                    store_rows(cns.rearrange("(s r) d -> s r d", r=30)[:, 29, :], lambda m: gs[:, m, 0:NS], NS)
                    dma("sp", cns.rearrange("(s r) d -> s r d", r=30)[:, 0:29, :],
                        scv.rearrange("(s r) d -> s r d", r=30)[:, 1:30, :])
                for (c0, n) in ctiles:
                    ln(yT, c0, n, "c_ln_g", "c_ln_b", 0, out_b=ysT, func=AF.Silu)
                yield
                proj([(wv(c_w_out), 0, 8, lambda k, c0, n: ysT[:, k, c0:c0 + n])], 8, evac_z, ctiles, after_ct=ln1)
                yield

            else:
                qext = lview(0, [8, 2 + 512], BF16)
                o1 = 8 * (2 + 512) * 2
                tT = lview(o1, [8, WL], BF16)
                o2 = o1 + 8 * WL * 2
                D3 = aview(abase + o2, [3, 128], BF16)
                o3 = o2 + 3 * 128 * 2
                stT = aview(abase + o3, [8, 32], F32)
                qs = smallf
                if first:
                    memset(qext[:, :, co:co + 2], 0.0)
                else:
                    cp(qext[:, :, co:co + 2], qcar[:, :, :])
                if has_s:
                    sst = ostgn()
                    dma("sp", sst[0:32, :], ssc[:, :])
                    for hh in range(2):
                        bk = psn()
                        for q in range(4):
                            tr(PS[:, bk, q * 32:(q + 1) * 32], sst[0:32, (hh * 4 + q) * 128:(hh * 4 + q + 1) * 128], ident[0:32, 0:32])
                        cp(stT[:, hh * 4:(hh + 1) * 4, :], PS[:, bk, 0:128].rearrange("p (a b) -> p a b", a=4))
                W = wv(d_w_in)

                def evac_q(m, c0, n, pc, ph):
                    ct_ = tmpn()
                    act(ct_[:, 0:n], pc, AF.Copy)
                    if n == 512:
                        tt(qext[:, m, 2 + c0:2 + c0 + n], ct_[:, 0:n], ph, ALU.mult)
                        if has_s:
                            tt(tails[:, m, 0:2], ct_[:, 510:512], ph[:, 510:512], ALU.mult)
                    else:
                        tt(qs[:, m, 0:n], ct_[:, 0:n], ph, ALU.mult)
                proj([(W, 1024, 8, rhs_x), (W, 2048, 8, rhs_x)], 8, evac_q, ctiles)
                cp(qcar[:, :, :], qext[:, :, co + 512:co + 512 + 2])
                yield
                if has_s:
                    store_rows(scp[:, :], lambda m: tails[:, m, 0:2], 2)
                    store_rows(scs.rearrange("(s r) d -> s r d", r=2)[:, 1, :], lambda m: qs[:, m, 0:NS], NS)
                    dma("sp", scs.rearrange("(s r) d -> s r d", r=2)[:, 0:1, :],
                        ssc.rearrange("(s r) d -> s r d", r=2)[:, 1:2, :])
                wb = None
                for m in range(8):
                    if m % 4 == 0:
                        wb = wload(W[:, :, m * 128:(m + 4) * 128])
                    for k in range(3):
                        ts(D3[:, k, :], identb[:], pcol("d_w_conv", k * 8 + m), None, ALU.mult)
                    for (c0, n) in ctiles:
                        bkb = psn()
                        for k in range(8):
                            mm(PS[:, bkb, 0:n], wb[:, k, (m % 4) * 128:(m % 4 + 1) * 128], xTb[:, k, c0:c0 + n], k == 0, k == 7)
                        bt = tmpn()
                        act(bt[:, 0:n], PS[:, bkb, 0:n], AF.Copy)
                        if n == 512:
                            bky = psn()
                            for k in range(3):
                                mm(PS[:, bky, 0:n], D3[:, k, :], qext[:, m, c0 + k:c0 + k + n], k == 0, k == 2)
                            tt(tT[:, m, c0:c0 + n], bt[:, 0:n], PS[:, bky, 0:n], ALU.mult)
                        else:
                            y_ = small2[:, m, 0:NS]
                            st3 = stT[:, m, :].rearrange("p (s r) -> p s r", r=2)
                            ts(y_, st3[:, :, 0], pcol("d_w_conv", 0 * 8 + m), None, ALU.mult)
                            stt(y_, st3[:, :, 1], pcol("d_w_conv", 1 * 8 + m), y_, ALU.mult, ALU.add)
                            stt(y_, qs[:, m, 0:NS], pcol("d_w_conv", 2 * 8 + m), y_, ALU.mult, ALU.add)
                            tt(tT[:, m, c0:c0 + n], bt[:, 0:n], y_, ALU.mult)
                        advance()
                yield
                proj([(wv(d_w_out), 0, 8, lambda k, c0, n: tT[:, k, c0:c0 + n])], 8, evac_z, ctiles, after_ct=ln1)
                yield

            if stage < 2:
                flush()
                continue
            hid = lview(0, [32, WL], BF16)

            def evac_hid(m, c0, n, ps_):
                r = tmpn()
                act(r[:, 0:n], ps_, AF.Relu)
                tt(hid[:, m, c0:c0 + n], r[:, 0:n], ps_, ALU.mult)
            proj([(wv(w_up[l]), 0, 8, rhs_x)], 32, evac_hid, ctiles)
            yield
            proj([(wv(w_down[l]), 0, 32, lambda k, c0, n: hid[:, k, c0:c0 + n])], 8, evac_z, ctiles, tile_chunks=1,
                 after_ct=lambda c0, n, l=l: ln(xT, c0, n, "ln2_g", "ln2_b", l * 8, out_f=xT, out_b=xTb))
            yield
            if stage < 3:
                flush()
                continue
            pnext = load_p(l + 1) if l + 1 < nlayers else None

            def evac_ple(m, c0, n, pgate, pproj):
                sg = tmpn()
                act(sg[:, 0:n], pgate, AF.Sigmoid)
                t2 = tmpn()
                tt(t2[:, 0:n], sg[:, 0:n], pproj, ALU.mult)
                tt(xT[:, m, c0:c0 + n], xT[:, m, c0:c0 + n], t2[:, 0:n], ALU.add)
            proj([(wv(w_gate[l]), 0, 8, rhs_x), (wv(w_proj[l]), 0, 2, lambda k, c0, n: pT[:, k, c0:c0 + n])],
                 8, evac_ple, ctiles,
                 after_ct=(lambda c0, n: act(xTb[:, :, c0:c0 + n], xT[:, :, c0:c0 + n], AF.Copy)) if l < nlayers - 1 else None)
            if pnext is not None:
                trans_p(pnext)
            yield

        flush()
        for j in jr:
            store_rows(yp[tok0 + j * 128: tok0 + (j + 1) * 128, :], lambda m, j=j: xT[:, m, j * 128:(j + 1) * 128], 128)
        if has_s:
            store_rows(ys[:, :], lambda m: xT[:, m, NPB:NPB + NS], NS)

    active = [lane_body(0, 0), lane_body(0, 1)]
    nextblk = [1, 1]
    for _ in range(SKEW - 1):
        next(active[0])
    while active[0] is not None or active[1] is not None:
        for i in (0, 1):
            g_ = active[i]
            if g_ is None:
                continue
            try:
                next(g_)
            except StopIteration:
                if nextblk[i] < 2:
                    active[i] = lane_body(nextblk[i], i)
                    nextblk[i] += 1
                else:
                    active[i] = None
    flush()
    return finish()


DBG = set()
_CONST = {}


def _consts():
    if _CONST:
        return _CONST
    ident = np.eye(128, dtype=np.float32)
    ii = np.arange(128)
    mask = (ii[None, :] <= ii[:, None]).astype(np.float32)
    invc = np.zeros((128, 64), np.float32)
    for g, w in enumerate((2, 4, 8, 16)):
        for t in range(16):
            invc[:, g * 16 + t] = 1.0 / min(w, t + 1)
    sel = np.zeros((120, 32), np.float32)
    for g, w in enumerate((2, 4, 8, 16)):
        for r in range(120):
            s, rr = divmod(r, 15)
            if rr >= 15 - (w - 1):
                sel[r, g * 8 + s] = 1.0
    _CONST.update(ident=ident, mask=mask, invc=invc, sel=sel)
    return _CONST


_NC_CACHE = {}


def kernel(x_prompt, x_sample, state_pool, state_conv, state_shortconv, p_prompt, p_sample,
           a_w_in, a_b_in, a_ln_g, a_ln_b, a_w_s, a_b_s, a_w_out,
           b_w_in, b_w_grp, b_scale, b_w_out,
           c_w_in, c_b_in, c_w_dw, c_b_dw, c_ln_g, c_ln_b, c_w_out,
           d_w_in, d_w_conv, d_w_out,
           ln1_g, ln1_b, ln2_g, ln2_b, mlp_w_up, mlp_w_down, ple_w_proj, ple_w_gate, _nlayers=4, _stage=9):
    f = lambda a: np.ascontiguousarray(np.asarray(a, dtype=np.float32))
    vecs = dict(a_b_in=a_b_in, a_ln_g=a_ln_g, a_ln_b=a_ln_b, b_scale=b_scale, c_b_in=c_b_in, c_b_dw=c_b_dw,
                c_ln_g=c_ln_g, c_ln_b=c_ln_b, ln1_g=ln1_g, ln1_b=ln1_b, ln2_g=ln2_g, ln2_b=ln2_b,
                c_w_dw=c_w_dw, d_w_conv=d_w_conv)
    pvec = np.concatenate([f(vecs[n]).reshape(-1, 128) for n, _ in PV_ORDER], axis=0)
    assert pvec.shape == (PV_ROWS, 128)
    cs = _consts()
    shared = dict(pvec=pvec, ident=cs["ident"], mask=cs["mask"], invc=cs["invc"], sel=cs["sel"],
                  a_w_in=f(a_w_in), a_w_s=f(a_w_s), a_b_s=f(a_b_s).reshape(1, D), a_w_out=f(a_w_out),
                  b_w_in=f(b_w_in), b_w_grp=f(b_w_grp), b_w_out=f(b_w_out),
                  c_w_in=f(c_w_in), c_w_out=f(c_w_out), d_w_in=f(d_w_in), d_w_out=f(d_w_out),
                  mlp_w_up=f(mlp_w_up), mlp_w_down=f(mlp_w_down), ple_w_proj=f(ple_w_proj), ple_w_gate=f(ple_w_gate))
    xp_ = f(x_prompt); xs_ = f(x_sample).reshape(128, D)
    pp_ = f(p_prompt); ps_ = f(p_sample).reshape(4, 128, 256)
    sp_ = f(state_pool); sc_ = f(state_conv); ss_ = f(state_shortconv)
    in_maps = []
    for c in range(NCORES):
        sl = slice(c * NS, (c + 1) * NS)
        m = dict(shared)
        m.update(xp=xp_[c], xs=xs_[sl], pp=np.ascontiguousarray(pp_[:, c]), psm=np.ascontiguousarray(ps_[:, sl]),
                 spool=sp_[sl].reshape(NS * 15, D), scv=sc_[sl].reshape(NS * 30, D), ssc=ss_[sl].reshape(NS * 2, D))
        in_maps.append(m)
    if (_nlayers, _stage) not in _NC_CACHE:
        _NC_CACHE[(_nlayers, _stage)] = build(_nlayers, _stage)
    nc = _NC_CACHE[(_nlayers, _stage)]
    res = run_bass_kernel_spmd(nc, in_maps, core_ids=list(range(NCORES)))
    R = res.results
    cat = lambda k: np.stack([np.asarray(r[k], dtype=np.float32) for r in R], axis=0)
    y_prompt = cat("yp")
    y_sample = cat("ys").reshape(128, 1, D)
    chunkv_prompt = cat("cvp")
    chunkv_sample = cat("cvs").reshape(128, 1, D)
    pool_prompt = cat("plp")
    pool_sample = cat("pls").reshape(128, 15, D)
    conv_prompt = cat("cnp")
    conv_sample = cat("cns").reshape(128, 30, D)
    sconv_prompt = cat("scp")
    sconv_sample = cat("scs").reshape(128, 2, D)
    return (y_prompt, y_sample, chunkv_prompt, chunkv_sample, pool_prompt, pool_sample,
            conv_prompt, conv_sample, sconv_prompt, sconv_sample)
```
